# Optimizing a Trainium2 kernel written in Bass

```python
import math
import jax, jax.numpy as jnp
from jax import lax
import numpy as np

D_MODEL = 1024
BATCH = 16
SEQ = 2048
DEPTH = 1
DEC_BATCH = 32
DEC_SEQ = 32
PAST_LEN = 2048

CHUNK = 64
CONV_WIDTH = 31
CONV_DIM = 512
ATT_HEADS = 4
ATT_HEAD_DIM = 64
ATT_V_DIM = 2 * ATT_HEAD_DIM
ATT_DIM = ATT_HEADS * ATT_V_DIM
QK_DIM = ATT_HEADS * 2 * ATT_HEAD_DIM
MIX_DIM = CONV_DIM + ATT_DIM
IN_DIM = 2 * CONV_DIM + 2 * QK_DIM + ATT_DIM
ROPE_THETA = 10000.0
Q_BLOCK = 128
PEER_HEADS = 8
PEER_KEYS = 128
PEER_N = PEER_KEYS * PEER_KEYS
PEER_QDIM = 256
PEER_HALF = PEER_QDIM // 2
PEER_TOPK = 16
PEER_TOK_BLOCK = 256
LN_EPS = 1e-5
DEEPNORM_ALPHA = (2 * DEPTH) ** 0.25
DEEPNORM_BETA = (8 * DEPTH) ** -0.25

kernel_name = 'hymba_conformer_diffattn_peer_stream_step'


def layer_norm(x, g, b):
    x32 = x.astype(jnp.float32)
    mu = jnp.mean(x32, -1, keepdims=True)
    var = jnp.mean(jnp.square(x32 - mu), -1, keepdims=True)
    return ((x32 - mu) * lax.rsqrt(var + LN_EPS) * g.astype(jnp.float32) + b.astype(jnp.float32)).astype(x.dtype)


def rope(x, pos):
    half = ATT_HEAD_DIM // 2
    inv = 1.0 / (ROPE_THETA ** (jnp.arange(half, dtype=jnp.float32) / half))
    ang = pos.astype(jnp.float32)[:, None] * inv[None, :]
    cos = jnp.cos(ang)[:, None, None, :]
    sin = jnp.sin(ang)[:, None, None, :]
    x32 = x.astype(jnp.float32)
    x1, x2 = x32[..., :half], x32[..., half:]
    return jnp.concatenate([x1 * cos - x2 * sin, x2 * cos + x1 * sin], -1).astype(x.dtype)


def diff_attend(q, k, v, q_pos, k_pos, lam, subln_g, lam_init):
    s = jnp.einsum('bqhpd,bkhpd->bhpqk', q, k).astype(jnp.float32) * (ATT_HEAD_DIM ** -0.5)
    allowed = (k_pos[None, :] // CHUNK) <= (q_pos[:, None] // CHUNK)
    s = jnp.where(allowed, s, -jnp.inf)
    p = jax.nn.softmax(s, axis=-1)
    a = p[:, :, 0] - lam * p[:, :, 1]
    o = jnp.einsum('bhqk,bkhe->bqhe', a.astype(v.dtype), v).astype(jnp.float32)
    o = o * lax.rsqrt(jnp.mean(jnp.square(o), -1, keepdims=True) + LN_EPS) * subln_g.astype(jnp.float32)
    return (o * (1.0 - lam_init)).astype(v.dtype)


def prompt_attention(q, k, v, lam, subln_g, lam_init):
    B, S = q.shape[0], q.shape[1]
    nb = S // Q_BLOCK
    qb = q.reshape(B, nb, Q_BLOCK, ATT_HEADS, 2, ATT_HEAD_DIM).swapaxes(0, 1)
    k_pos = jnp.arange(S)

    def one(args):
        qblk, i = args
        q_pos = i * Q_BLOCK + jnp.arange(Q_BLOCK)
        return diff_attend(qblk, k, v, q_pos, k_pos, lam, subln_g, lam_init)

    ob = lax.map(one, (qb, jnp.arange(nb)))
    return ob.swapaxes(0, 1).reshape(B, S, ATT_HEADS, ATT_V_DIM)


def causal_dwconv(u, hist, w, b):
    xp = jnp.concatenate([hist, u], axis=1)
    y = lax.conv_general_dilated(xp, w[:, None, :], window_strides=(1,), padding='VALID',
                                 dimension_numbers=('NWC', 'WIO', 'NWC'), feature_group_count=CONV_DIM)
    return y + b, xp[:, -(CONV_WIDTH - 1):]


def peer(h, w_query, sub_keys, u_tab, v_tab):
    B, L, D = h.shape
    T = B * L
    nb = -(-T // PEER_TOK_BLOCK)
    t = jnp.pad(h.reshape(T, D), ((0, nb * PEER_TOK_BLOCK - T), (0, 0))).reshape(nb, PEER_TOK_BLOCK, D)

    def one(tb):
        q = (tb @ w_query).reshape(PEER_TOK_BLOCK, PEER_HEADS, 2, PEER_HALF)
        s = jnp.einsum('thpd,hpkd->thpk', q, sub_keys).astype(jnp.float32)
        sv, si = lax.top_k(s, PEER_TOPK)
        cand = (sv[:, :, 0, :, None] + sv[:, :, 1, None, :]).reshape(PEER_TOK_BLOCK, PEER_HEADS, PEER_TOPK * PEER_TOPK)
        cidx = (si[:, :, 0, :, None] * PEER_KEYS + si[:, :, 1, None, :]).reshape(PEER_TOK_BLOCK, PEER_HEADS, PEER_TOPK * PEER_TOPK)
        fv, fp = lax.top_k(cand, PEER_TOPK)
        e = jnp.take_along_axis(cidx, fp, axis=-1)
        g = jax.nn.softmax(fv, axis=-1)
        a = jnp.einsum('td,thkd->thk', tb, u_tab[e]).astype(jnp.float32)
        a = jax.nn.gelu(a, approximate=False) * g
        return jnp.einsum('thk,thkd->td', a.astype(tb.dtype), v_tab[e])

    out = lax.map(one, t).reshape(nb * PEER_TOK_BLOCK, D)[:T]
    return out.reshape(B, L, D)


def layer_forward(x, c, conv_hist, k_past, v_past, lw, lam_init):
    B, L, _ = x.shape
    mod = jax.nn.silu(c) @ lw['w_ada'] + lw['b_ada']
    sh1, sc1, g1, sh2, sc2, g2 = jnp.split(mod[:, None, :], 6, axis=-1)
    h = x * (1 + sc1) + sh1
    z = h @ lw['w_in'] + lw['b_in']
    conv_a, conv_b, q, k, v = jnp.split(z, [CONV_DIM, 2 * CONV_DIM, 2 * CONV_DIM + QK_DIM, 2 * CONV_DIM + 2 * QK_DIM], axis=-1)
    u = conv_a * jax.nn.sigmoid(conv_b)
    dw, conv_state = causal_dwconv(u, conv_hist, lw['w_dw'], lw['b_dw'])
    conv_out = jax.nn.silu(layer_norm(dw, lw['conv_ln_g'], lw['conv_ln_b']))
    past = 0 if k_past is None else k_past.shape[1]
    q_pos = past + jnp.arange(L)
    q = rope(q.reshape(B, L, ATT_HEADS, 2, ATT_HEAD_DIM), q_pos)
    k = rope(k.reshape(B, L, ATT_HEADS, 2, ATT_HEAD_DIM), q_pos)
    v = v.reshape(B, L, ATT_HEADS, ATT_V_DIM)
    f32 = jnp.float32
    lam = (jnp.exp(jnp.sum(lw['lam_q1'].astype(f32) * lw['lam_k1'].astype(f32)))
           - jnp.exp(jnp.sum(lw['lam_q2'].astype(f32) * lw['lam_k2'].astype(f32))) + lam_init)
    if k_past is None:
        att = prompt_attention(q, k, v, lam, lw['subln_g'], lam_init)
    else:
        k_all = jnp.concatenate([k_past, k], axis=1)
        v_all = jnp.concatenate([v_past, v], axis=1)
        att = diff_attend(q, k_all, v_all, q_pos, jnp.arange(past + L), lam, lw['subln_g'], lam_init)
    mix = jnp.concatenate([conv_out, att.reshape(B, L, ATT_DIM)], axis=-1)
    x = layer_norm(DEEPNORM_ALPHA * x + g1 * (mix @ lw['w_out'] + lw['b_out']), lw['ln1_g'], lw['ln1_b'])
    h2 = x * (1 + sc2) + sh2
    ff = peer(h2, lw['w_query'], lw['sub_keys'], lw['u_tab'], lw['v_tab'])
    x = layer_norm(DEEPNORM_ALPHA * x + g2 * ff, lw['ln2_g'], lw['ln2_b'])
    return x, conv_state, k, v


def setup_inputs(seed: int = 0) -> dict:
    key = jax.random.key(seed)
    ks = jax.random.split(key, 40)
    n = lambda i, shape, s: jax.random.normal(ks[i], shape, jnp.float32) * s
    D = D_MODEL
    return {
        'x_prompt': n(0, (BATCH, SEQ, D), 1.0),
        'x_sample': n(1, (DEC_BATCH, DEC_SEQ, D), 1.0),
        'cache_k': n(2, (DEPTH, DEC_BATCH, PAST_LEN, ATT_HEADS, 2, ATT_HEAD_DIM), 1.0),
        'cache_v': n(3, (DEPTH, DEC_BATCH, PAST_LEN, ATT_HEADS, ATT_V_DIM), 1.0),
        'cache_conv': n(4, (DEPTH, DEC_BATCH, CONV_WIDTH - 1, CONV_DIM), 0.5),
        'c_prompt': n(5, (BATCH, D), 1.0),
        'c_sample': n(6, (DEC_BATCH, D), 1.0),
        'w_ada': n(7, (DEPTH, D, 6 * D), 0.5 * D ** -0.5),
        'b_ada': n(8, (DEPTH, 6 * D), 0.01),
        'w_in': n(9, (DEPTH, D, IN_DIM), D ** -0.5),
        'b_in': n(10, (DEPTH, IN_DIM), 0.01),
        'w_dw': n(11, (DEPTH, CONV_WIDTH, CONV_DIM), CONV_WIDTH ** -0.5),
        'b_dw': n(12, (DEPTH, CONV_DIM), 0.01),
        'conv_ln_g': 1.0 + n(13, (DEPTH, CONV_DIM), 0.05),
        'conv_ln_b': n(14, (DEPTH, CONV_DIM), 0.01),
        'lam_q1': n(15, (DEPTH, ATT_HEAD_DIM), 0.1),
        'lam_k1': n(16, (DEPTH, ATT_HEAD_DIM), 0.1),
        'lam_q2': n(17, (DEPTH, ATT_HEAD_DIM), 0.1),
        'lam_k2': n(18, (DEPTH, ATT_HEAD_DIM), 0.1),
        'subln_g': 1.0 + n(19, (DEPTH, ATT_V_DIM), 0.05),
        'w_out': n(20, (DEPTH, MIX_DIM, D), DEEPNORM_BETA * MIX_DIM ** -0.5),
        'b_out': n(21, (DEPTH, D), 0.01),
        'ln1_g': 1.0 + n(22, (DEPTH, D), 0.05),
        'ln1_b': n(23, (DEPTH, D), 0.01),
        'w_query': n(24, (DEPTH, D, PEER_HEADS * PEER_QDIM), D ** -0.5),
        'sub_keys': n(25, (DEPTH, PEER_HEADS, 2, PEER_KEYS, PEER_HALF), PEER_HALF ** -0.5),
        'u_tab': n(26, (DEPTH, PEER_N, D), D ** -0.5),
        'v_tab': n(27, (DEPTH, PEER_N, D), DEEPNORM_BETA * D ** -0.5),
        'ln2_g': 1.0 + n(28, (DEPTH, D), 0.05),
        'ln2_b': n(29, (DEPTH, D), 0.01),
    }


def reference(x_prompt, x_sample, cache_k, cache_v, cache_conv, c_prompt, c_sample,
              w_ada, b_ada, w_in, b_in, w_dw, b_dw, conv_ln_g, conv_ln_b,
              lam_q1, lam_k1, lam_q2, lam_k2, subln_g, w_out, b_out, ln1_g, ln1_b,
              w_query, sub_keys, u_tab, v_tab, ln2_g, ln2_b):
    xp, xs = x_prompt, x_sample
    kp_l, vp_l, cp_l, ks_l, vs_l, cs_l = [], [], [], [], [], []
    for l in range(DEPTH):
        lam_init = 0.8 - 0.6 * math.exp(-0.3 * l)
        lw = {'w_ada': w_ada[l], 'b_ada': b_ada[l], 'w_in': w_in[l], 'b_in': b_in[l],
              'w_dw': w_dw[l], 'b_dw': b_dw[l], 'conv_ln_g': conv_ln_g[l], 'conv_ln_b': conv_ln_b[l],
              'lam_q1': lam_q1[l], 'lam_k1': lam_k1[l], 'lam_q2': lam_q2[l], 'lam_k2': lam_k2[l],
              'subln_g': subln_g[l], 'w_out': w_out[l], 'b_out': b_out[l],
              'ln1_g': ln1_g[l], 'ln1_b': ln1_b[l], 'w_query': w_query[l], 'sub_keys': sub_keys[l],
              'u_tab': u_tab[l], 'v_tab': v_tab[l], 'ln2_g': ln2_g[l], 'ln2_b': ln2_b[l]}
        hist0 = jnp.zeros((xp.shape[0], CONV_WIDTH - 1, CONV_DIM), xp.dtype)
        xp, cp, kp, vp = layer_forward(xp, c_prompt, hist0, None, None, lw, lam_init)
        xs, cs, kn, vn = layer_forward(xs, c_sample, cache_conv[l], cache_k[l], cache_v[l], lw, lam_init)
        kp_l.append(kp); vp_l.append(vp); cp_l.append(cp)
        ks_l.append(kn); vs_l.append(vn); cs_l.append(cs)
    k_prompt = jnp.stack(kp_l)
    v_prompt = jnp.stack(vp_l)
    conv_prompt = jnp.stack(cp_l)
    k_sample = jnp.stack(ks_l)
    v_sample = jnp.stack(vs_l)
    conv_sample = jnp.stack(cs_l)
    return (xp, xs, k_prompt, v_prompt, conv_prompt, k_sample, v_sample, conv_sample)
```

```python
import math
from contextlib import ExitStack
import numpy as np
import concourse.bass as bass
import concourse.mybir as mybir
from concourse.bass_utils import run_bass_kernel_spmd

F32 = mybir.dt.float32
BF16 = mybir.dt.bfloat16
I32 = mybir.dt.int32
U32 = mybir.dt.uint32
ALU = mybir.AluOpType
AF = mybir.ActivationFunctionType
AX = mybir.AxisListType

NCORES = 8
D = 1024
SEQ = 2048
NPS = 2
NSS = 4
DSEQ = 32
PAST = 2048
CW = 31
HIST = CW - 1
ALPHA = 2.0 ** 0.25
LAM_INIT = 0.8 - 0.6 * math.exp(0.0)
EPS = 1e-5
NEXP = 16384
STAGE = 9
PEER_ON = True
CONV_PE = True
NRING = 8
DEBUG = False

ENGS = ['pe', 'act', 'dve', 'pool', 'sp']
SAME_SYNC = {'pe': False, 'act': True, 'dve': True, 'pool': True, 'sp': False}


class Buf:
    __slots__ = ('name', 'w', 'r')

    def __init__(self, name=''):
        self.name = name
        self.w = None
        self.r = {}


class Tracker:
    def __init__(self, nc, es, n_dma_sems=32):
        self.nc = nc
        self.streams = {e: [] for e in ENGS}
        self.sems = {}
        for e in ['pe', 'act', 'dve', 'pool']:
            self.sems[e] = es.enter_context(nc.semaphore('s_' + e))
        for i in range(n_dma_sems):
            self.sems[('d', i)] = es.enter_context(nc.semaphore('d%d' % i))
        self.n_dma = n_dma_sems
        self.cnt = {k: 0 for k in self.sems}
        self.known = {e: {} for e in ENGS}
        self.rr = {'sp': 0, 'pool': 0, 'act': 0}
        self.nops = {e: 0 for e in ENGS}

    def _need(self, eng, key, val, force=False):
        if val <= 0:
            return
        if key == eng and not (SAME_SYNC[eng] or force):
            return
        if self.known[eng].get(key, 0) >= val:
            return
        self.known[eng][key] = val
        sem = self.sems[key]
        self.streams[eng].append(L('wait_ge', sem, val))

    def _deps(self, eng, reads, writes):
        for b in reads:
            if b.w is not None:
                self._need(eng, *b.w)
        for b in writes:
            if b.w is not None:
                self._need(eng, *b.w)
            for k, v in b.r.items():
                self._need(eng, k, v)

    def _mark(self, key, v, reads, writes):
        for b in reads:
            if b.r.get(key, 0) < v:
                b.r[key] = v
        for b in writes:
            b.w = (key, v)
            b.r = {}

    def op(self, eng, fn, reads=(), writes=()):
        self._deps(eng, reads, writes)
        self.cnt[eng] += 1
        v = self.cnt[eng]
        sem = self.sems[eng]
        self.streams[eng].append(lambda e, fn=fn, sem=sem: fn(e).then_inc(sem, 1))
        self.nops[eng] += 1
        self._mark(eng, v, reads, writes)

    def dma(self, q, fn, reads=(), writes=()):
        half = self.n_dma // 2
        i = self.rr[q]
        self.rr[q] = (i + 1) % half
        key = ('d', i + (half if q == 'pool' else 0))
        prev = self.cnt[key]
        self._need(q, key, prev)
        self._deps(q, reads, writes)
        self.cnt[key] = prev + 16
        sem = self.sems[key]
        self.streams[q].append(lambda e, fn=fn, sem=sem: fn(e).then_inc(sem, 16))
        self.nops[q] += 1
        self._mark(key, prev + 16, reads, writes)

    def barrier(self):
        for e in ENGS:
            for key, v in self.cnt.items():
                if key != e:
                    self._need(e, key, v)

    def finish(self):
        for key, v in self.cnt.items():
            self._need('sp', key, v)

    def emit(self, block):
        st = self.streams

        @block.sync
        def _(e):
            for f in st['sp']:
                f(e)

        @block.tensor
        def _(e):
            for f in st['pe']:
                f(e)

        @block.scalar
        def _(e):
            for f in st['act']:
                f(e)

        @block.vector
        def _(e):
            for f in st['dve']:
                f(e)

        @block.gpsimd
        def _(e):
            for f in st['pool']:
                f(e)


def L(name, *args, **kw):
    return lambda e: getattr(e, name)(*args, **kw)


class Arena:
    def __init__(self, ap_all, nwords):
        self.ap = ap_all
        self.n = nwords
        self.top = 0
        self.marks = []

    def alloc(self, shape, dt=F32):
        per = 1
        for s in shape[1:]:
            per *= s
        nb = per * (4 if dt in (F32, I32, U32) else 2)
        nw = (nb + 3) // 4
        nw = (nw + 7) // 8 * 8
        off = self.top
        self.top += nw
        assert self.top <= self.n, "arena overflow %d > %d" % (self.top, self.n)
        v = self.ap[0:shape[0], off:off + nw]
        if dt != F32:
            v = v.bitcast(dt)
        v = v[:, 0:per]
        if len(shape) == 3:
            v = v.rearrange("p (a b) -> p a b", a=shape[1])
        elif len(shape) == 4:
            v = v.rearrange("p (a b c) -> p a b c", a=shape[1], b=shape[2])
        return v

    def mark(self):
        self.marks.append(self.top)

    def release(self):
        self.top = self.marks.pop()


def build_program():
    nc = bass.Bass("TRN2", target_bir_lowering=False)

    def din(name, shape, dt=F32):
        return nc.dram_tensor(name, list(shape), dt, kind="ExternalInput").ap()

    def dout(name, shape, dt=F32):
        return nc.dram_tensor(name, list(shape), dt, kind="ExternalOutput").ap()

    xp = din("xp", [NPS * SEQ, D])
    xs = din("xs", [128, D])
    ck = din("ck", [NSS * PAST, 512])
    cv = din("cv", [NSS * PAST, 512])
    cc = din("cc", [NSS * HIST, 512])
    c6 = din("c6", [6, D])
    w_ada = din("w_ada", [D, 6 * D])
    b_ada = din("b_ada", [1, 6 * D])
    w_in = din("w_in", [D, 2560])
    b_in = din("b_in", [1, 2560])
    w_dw = din("w_dw", [CW, 512])
    b_dw = din("b_dw", [1, 512])
    cln_g = din("cln_g", [1, 512])
    cln_b = din("cln_b", [1, 512])
    lamv = din("lamv", [4, 64])
    subln_g = din("subln_g", [1, 128])
    w_out = din("w_out", [D, D])
    b_out = din("b_out", [1, D])
    ln1_g = din("ln1_g", [1, D])
    ln1_b = din("ln1_b", [1, D])
    w_query = din("w_query", [D, 2048])
    sub_keys = din("sub_keys", [16 * 128, 128])
    u_tab = din("u_tab", [NEXP, D])
    v_tab = din("v_tab", [NEXP, D])
    ln2_g = din("ln2_g", [1, D])
    ln2_b = din("ln2_b", [1, D])
    ident_d = din("ident", [128, 128])
    iota_d = din("iota16", [128, 16])
    cosp_d = din("cosp", [SEQ, 32])
    sinp_d = din("sinp", [SEQ, 32])
    coss_d = din("coss", [128, 32])
    sins_d = din("sins", [128, 32])

    y_p = dout("y_p", [NPS * SEQ, D])
    y_s = dout("y_s", [128, D])
    k_p = dout("k_p", [NPS * SEQ, 512])
    v_p = dout("v_p", [NPS * SEQ, 512])
    conv_p = dout("conv_p", [NPS * HIST, 512])
    k_s = dout("k_s", [128, 512])
    v_s = dout("v_s", [128, 512])
    conv_s = dout("conv_s", [NSS * HIST, 512])

    if DEBUG:
        dbg_h2 = dout("dbg_h2", [128, D]); dbg_ff = dout("dbg_ff", [128, D]); dbg_e = dout("dbg_e", [128, 128], I32); dbg_g = dout("dbg_g", [128, 128])
    modscr = nc.dram_tensor("modscr", [6, 4096], F32, kind="Internal").ap()
    x1scr = nc.dram_tensor("x1scr", [NPS * SEQ + 128, D], F32, kind="Internal").ap()

    with ExitStack() as es:
        T = Tracker(nc, es, n_dma_sems=32)
        AW = 52600
        arena_t = es.enter_context(nc.sbuf_tensor("arena", [128, AW], F32))
        A = Arena(arena_t[:, :], AW)
        ps = es.enter_context(nc.psum_tensor("ps", [128, 8, 512], F32))
        PB = [Buf("ps%d" % i) for i in range(8)]

        def op(eng, fn, r=(), w=()):
            T.op(eng, fn, r, w)

        def dma(fn, r=(), w=(), q='sp'):
            T.dma(q, fn, r, w)

        def load(out_ap, in_ap, wb, q='sp'):
            T.dma(q, L('dma_start', out=out_ap, in_=in_ap), (), [wb])

        def store(out_ap, in_ap, rb, wbufs=()):
            T.dma('sp', L('dma_start', out=out_ap, in_=in_ap), [rb], list(wbufs))

        def mm(out_ap, lhsT, rhs, start, stop, r, w):
            T.op('pe', L('matmul', out_ap, lhsT=lhsT, rhs=rhs, start=start, stop=stop), r, w)

        def tr(out_ap, in_ap, k, r, w):
            T.op('pe', L('transpose', out_ap, in_ap, ident[0:k, 0:k]), list(r) + [Bc], w)

        def ln_tail(sm1, Bsm, tw, Bt, rw_, Br_, out_ap, Bout, gbc, bbc, Bg):
            op('dve', L('memset', sm1[:, 1:2], 0.0), [Bsm], [Bsm])
            op('dve', L('scalar_tensor_tensor', out=tw, in0=rw_, scalar=1.0, in1=rw_, op0=ALU.mult, op1=ALU.mult, accum_out=sm1[:, 1:2]),
               [Br_, Bsm], [Bt, Bsm])
            op('dve', L('tensor_scalar', out=sm1[:, 2:3], in0=sm1[:, 0:1], scalar1=-1.0 / D, scalar2=None, op0=ALU.mult), [Bsm], [Bsm])
            op('dve', L('tensor_tensor', out=sm1[:, 3:4], in0=sm1[:, 2:3], in1=sm1[:, 2:3], op=ALU.mult), [Bsm], [Bsm])
            op('dve', L('scalar_tensor_tensor', out=sm1[:, 4:5], in0=sm1[:, 1:2], scalar=1.0 / D, in1=sm1[:, 3:4], op0=ALU.mult, op1=ALU.subtract),
               [Bsm], [Bsm])
            op('act', L('activation', out=sm1[:, 5:6], in_=sm1[:, 4:5], func=AF.Sqrt, bias=EPS, scale=1.0), [Bsm], [Bsm])
            op('dve', L('reciprocal', out=sm1[:, 6:7], in_=sm1[:, 5:6]), [Bsm], [Bsm])
            op('dve', L('tensor_scalar', out=tw, in0=rw_, scalar1=sm1[:, 2:3], scalar2=sm1[:, 6:7], op0=ALU.add, op1=ALU.mult), [Br_, Bsm, Bt], [Bt])
            op('pool', L('tensor_tensor', out=tw, in0=tw, in1=gbc, op=ALU.mult), [Bt, Bg], [Bt])
            op('pool', L('tensor_tensor', out=out_ap, in0=tw, in1=bbc, op=ALU.add), [Bt, Bg], [Bout])

        ident = A.alloc([128, 128])
        onesdiv = A.alloc([128, 128])
        iota16 = A.alloc([128, 16])
        smallT = A.alloc([128, 40])
        wdwT = A.alloc([128, 4, CW])
        modT = A.alloc([128, 16, 6])
        lam_t = A.alloc([128, 4])
        gsub = A.alloc([128, 128])
        Bc = Buf("consts")
        Bmod = Buf("modT")
        load(ident, ident_d, Bc)
        load(iota16, iota_d, Bc)
        op('dve', L('memset', onesdiv, 1.0 / 512.0), (), [Bc])

        A.mark()
        stg = A.alloc([128, 128])
        Bstg = Buf("stg")
        op('pool', L('memset', stg, 0.0), (), [Bstg])
        load(stg[0:16, :], b_ada[0:1, 0:2048].rearrange("o (c p) -> (o c) p", p=128), Bstg)
        load(stg[16:24, :], b_in[0:1, 0:1024].rearrange("o (c p) -> (o c) p", p=128), Bstg)
        load(stg[24:28, :], b_dw[0:1, :].rearrange("o (c p) -> (o c) p", p=128), Bstg)
        load(stg[28:32, :], cln_g[0:1, :].rearrange("o (c p) -> (o c) p", p=128), Bstg)
        load(stg[32:36, :], cln_b[0:1, :].rearrange("o (c p) -> (o c) p", p=128), Bstg)
        tr(ps[:, 0, 0:40], stg[0:40, :], 40, [Bstg], [PB[0]])
        op('dve', L('tensor_copy', out=smallT, in_=ps[:, 0, 0:40]), [PB[0]], [Bc])
        wdw_s = A.alloc([CW, 512])
        Bw = Buf("wdw_s")
        load(wdw_s, w_dw, Bw)
        for c in range(4):
            tr(ps[:, 1, c * 32:c * 32 + CW], wdw_s[0:CW, c * 128:(c + 1) * 128], CW, [Bw], [PB[1]])
        op('dve', L('tensor_copy', out=wdwT, in_=ps[:, 1, 0:128].rearrange("p (c j) -> p c j", c=4)[:, :, 0:CW]), [PB[1]], [Bc])
        lam_s = A.alloc([128, 4, 64])
        Bl = Buf("lam_s")
        for i in range(4):
            load(lam_s[:, i, :], lamv[i:i + 1, :].partition_broadcast(128), Bl)
        lam_j = A.alloc([128, 64])
        lam_a = A.alloc([128, 4])
        Blj = Buf("lamj")
        op('dve', L('memset', lam_a, 0.0), (), [Blj])
        for i in range(2):
            op('dve', L('scalar_tensor_tensor', out=lam_j, in0=lam_s[:, 2 * i, :], scalar=1.0, in1=lam_s[:, 2 * i + 1, :],
                                                             op0=ALU.mult, op1=ALU.mult, accum_out=lam_a[:, i:i + 1]), [Bl, Blj], [Blj])
        op('act', L('activation', out=lam_a[:, 2:4], in_=lam_a[:, 0:2], func=AF.Exp), [Blj], [Blj])
        op('dve', L('tensor_tensor', out=lam_t[:, 0:1], in0=lam_a[:, 2:3], in1=lam_a[:, 3:4], op=ALU.subtract), [Blj], [Bc])
        op('dve', L('tensor_scalar', out=lam_t[:, 0:1], in0=lam_t[:, 0:1], scalar1=float(LAM_INIT), scalar2=None, op0=ALU.add), [Bc], [Bc])
        op('dve', L('tensor_scalar', out=lam_t[:, 1:2], in0=lam_t[:, 0:1], scalar1=-1.0, scalar2=None, op0=ALU.mult), [Bc], [Bc])
        load(gsub, subln_g[0:1, :].partition_broadcast(128), Bc)
        op('dve', L('tensor_scalar', out=gsub, in0=gsub, scalar1=float(1.0 - LAM_INIT), scalar2=None, op0=ALU.mult), [Bc], [Bc])

        c6s = A.alloc([6, D])
        Bc6 = Buf("c6")
        load(c6s, c6, Bc6)
        op('act', L('activation', out=c6s, in_=c6s, func=AF.Silu), [Bc6], [Bc6])
        for c in range(8):
            tr(ps[:, 2, c * 6:(c + 1) * 6], c6s[0:6, c * 128:(c + 1) * 128], 6, [Bc6], [PB[2]])
        scT = A.alloc([128, 8, 6], BF16)
        BscT = Buf("scT")
        op('dve', L('tensor_copy', out=scT, in_=ps[:, 2, 0:48].rearrange("p (c s) -> p c s", c=8)), [PB[2]], [BscT])
        bada6 = A.alloc([6, 4096])
        Bb6 = Buf("bada6")
        load(bada6, b_ada[0:1, 2048:6144].partition_broadcast(6), Bb6)
        modtok = A.alloc([6, 4096])
        Bmt = Buf("modtok")
        wst = [A.alloc([128, 8, 512]) for _ in range(2)]
        wbf = [A.alloc([128, 8, 512], BF16) for _ in range(2)]
        Bwst = [Buf("wst0"), Buf("wst1")]
        Bwbf = [Buf("wbf0"), Buf("wbf1")]
        w_ada_v = w_ada.rearrange("(k p) n -> p k n", p=128)
        cast_eng = ['dve', 'pool']
        for cb in range(12):
            s = cb % 2
            load(wst[s], w_ada_v[:, :, cb * 512:(cb + 1) * 512], Bwst[s])
            if cast_eng[s] == 'dve':
                op('dve', L('tensor_copy', out=wbf[s], in_=wst[s]), [Bwst[s]], [Bwbf[s]])
            else:
                op('pool', L('tensor_copy', out=wbf[s], in_=wst[s]), [Bwst[s]], [Bwbf[s]])
            if cb < 4:
                for cc_ in range(4):
                    ch = cb * 4 + cc_
                    for k in range(8):
                        mm(ps[:, 3, ch * 6:(ch + 1) * 6], wbf[s][:, k, cc_ * 128:(cc_ + 1) * 128], scT[:, k, :], k == 0, k == 7,
                           [Bwbf[s], BscT], [PB[3]])
            else:
                pb = 4 + cb % 2
                for k in range(8):
                    mm(ps[0:6, pb, :], scT[:, k, :], wbf[s][:, k, :], k == 0, k == 7, [Bwbf[s], BscT], [PB[pb]])
                o0 = (cb - 4) * 512
                op('dve', L('tensor_tensor', out=modtok[:, o0:o0 + 512], in0=ps[0:6, pb, :], in1=bada6[:, o0:o0 + 512],
                                                                  op=ALU.add), [PB[pb], Bb6], [Bmt])
        for ch in range(16):
            if ch < 8:
                op('dve', L('tensor_scalar', out=modT[:, ch, :], in0=ps[:, 3, ch * 6:(ch + 1) * 6], scalar1=smallT[:, ch:ch + 1],
                                                           scalar2=None, op0=ALU.add), [PB[3], Bc], [Bmod])
            else:
                op('dve', L('tensor_scalar', out=modT[:, ch, :], in0=ps[:, 3, ch * 6:(ch + 1) * 6], scalar1=smallT[:, ch:ch + 1],
                                                           scalar2=1.0, op0=ALU.add, op1=ALU.add), [PB[3], Bc], [Bmod])
        op('dve', L('tensor_scalar', out=modtok[:, 2048:3072], in0=modtok[:, 2048:3072], scalar1=1.0, scalar2=None, op0=ALU.add), [Bmt], [Bmt])
        Bscr = Buf("modscr")
        store(modscr, modtok, Bmt, [Bscr])
        T.barrier()
        A.release()

        top_consts = A.top
        qT = A.alloc([128, 4, SEQ], BF16)
        kT = A.alloc([128, 4, SEQ], BF16)
        Vp = A.alloc([128, 16, 4, 130], BF16)
        u_ext = A.alloc([128, 4, HIST + SEQ])
        kTn = A.alloc([128, 4, 128], BF16)
        Vn = A.alloc([32, 4, 4, 130], BF16)
        BqT, BkT, BVp, Bu, Bmix, BkTn, BVn = [Buf(n) for n in ["qT", "kT", "Vp", "u", "mix", "kTn", "Vn"]]
        op('pool', L('memset', Vp[:, :, :, 128:130], 1.0), (), [BVp])
        op('pool', L('memset', Vn[:, :, :, 128:130], 1.0), (), [BVn])
        Bx1 = [Buf("x1s%d" % i) for i in range(33)]

        units = [dict(kind='p', idx=0), dict(kind='p', idx=1), dict(kind='s', idx=0)]
        if STAGE < 9:
            units = units[:1] + units[2:]

        for U in units:
            isP = U['kind'] == 'p'
            ntile = 16 if isP else 1
            ntok = ntile * 128
            TB = 512 if isP else 128
            nblk = ntok // TB
            if isP:
                x_src = xp[U['idx'] * SEQ:(U['idx'] + 1) * SEQ, :]
                y_dst = y_p[U['idx'] * SEQ:(U['idx'] + 1) * SEQ, :]
                k_dst = k_p[U['idx'] * SEQ:(U['idx'] + 1) * SEQ, :]
                v_dst = v_p[U['idx'] * SEQ:(U['idx'] + 1) * SEQ, :]
                segs = [(U['idx'], 0, SEQ)]
                tile0 = U['idx'] * 16
            else:
                x_src, y_dst, k_dst, v_dst = xs, y_s, k_s, v_s
                segs = [(2 + j, 32 * j, 32) for j in range(4)]
                tile0 = 32
            UW = (HIST + SEQ) if isP else 4 * 62

            def ucols(c, j0, n, b):
                if isP:
                    return u_ext[:, c, b * 512 + j0:b * 512 + j0 + 512]
                return u_ext[:, c, 0:248].rearrange("p (s t) -> p s t", s=4)[:, :, j0:j0 + 32]

            A.mark()
            A.mark()
            w_in_bf = A.alloc([128, 8, 2560], BF16)
            Bwin = Buf("w_in_bf")
            wst = [A.alloc([128, 8, 256]) for _ in range(2)]
            Bwst = [Buf("wst0"), Buf("wst1")]
            w_in_v = w_in.rearrange("(k p) n -> p k n", p=128)
            for cb in range(10):
                s = cb % 2
                load(wst[s], w_in_v[:, :, cb * 256:(cb + 1) * 256], Bwst[s])
                eng = 'dve' if s == 0 else 'pool'
                op(eng, L('tensor_copy', out=w_in_bf[:, :, cb * 256:(cb + 1) * 256], in_=wst[s]), [Bwst[s]], [Bwin])
            binbc = A.alloc([128, 1536])
            Bbin = Buf("binbc")
            load(binbc, b_in[0:1, 1024:2560].partition_broadcast(128), Bbin)
            cos_t = A.alloc([128, ntile, 32])
            sin_t = A.alloc([128, ntile, 32])
            Brope = Buf("rope")
            if isP:
                load(cos_t, cosp_d.rearrange("(t p) f -> p t f", p=128), Brope)
                load(sin_t, sinp_d.rearrange("(t p) f -> p t f", p=128), Brope)
            else:
                load(cos_t[:, 0, :], coss_d, Brope)
                load(sin_t[:, 0, :], sins_d, Brope)
            if isP:
                op('pool', L('memset', u_ext[:, :, 0:HIST], 0.0), (), [Bu])
            else:
                hst = A.alloc([HIST, 512])
                Bh = Buf("hst")
                for j in range(4):
                    load(hst, cc[j * HIST:(j + 1) * HIST, :], Bh)
                    for c in range(4):
                        tr(ps[:, 0, c * 32:c * 32 + HIST], hst[0:HIST, c * 128:(c + 1) * 128], HIST, [Bh], [PB[0]])
                    op('dve', L('tensor_copy', out=u_ext[:, :, 62 * j:62 * j + HIST],
                                                           in_=ps[:, 0, 0:128].rearrange("p (c t) -> p c t", c=4)[:, :, 0:HIST]), [PB[0]], [Bu])
            xst = [A.alloc([128, D]) for _ in range(2)]
            Bxst = [Buf("xst0"), Buf("xst1")]
            hTs = [A.alloc([128, 8, TB], BF16) for _ in range(2)]
            BhTs = [Buf("hT0"), Buf("hT1")]
            sig_t = A.alloc([128, 512])
            Bsig = Buf("sig")
            qb = A.alloc([128, 512]); kb = A.alloc([128, 512]); vb = A.alloc([128, 512])
            qr = A.alloc([128, 512]); kr = A.alloc([128, 512])
            rt1 = A.alloc([128, 256]); rt2 = A.alloc([128, 256])
            Bqb, Bkb, Bvb, Bqr, Bkr, Brt = [Buf(n) for n in ["qb", "kb", "vb", "qr", "kr", "rt"]]

            def rope(src, dst, Bs, Bd, t):
                s4 = src.rearrange("p (g two f) -> p g two f", g=8, two=2)
                d4 = dst.rearrange("p (g two f) -> p g two f", g=8, two=2)
                X1, X2 = s4[:, :, 0, :], s4[:, :, 1, :]
                O1, O2 = d4[:, :, 0, :], d4[:, :, 1, :]
                Cb = cos_t[:, t, :].unsqueeze(1).broadcast_to([128, 8, 32])
                Sb = sin_t[:, t, :].unsqueeze(1).broadcast_to([128, 8, 32])
                a1 = rt1.rearrange("p (g f) -> p g f", g=8)
                a2 = rt2.rearrange("p (g f) -> p g f", g=8)
                op('dve', L('tensor_tensor', out=a1, in0=X1, in1=Cb, op=ALU.mult), [Bs, Brope], [Brt])
                op('dve', L('tensor_tensor', out=a2, in0=X2, in1=Sb, op=ALU.mult), [Bs, Brope, Brt], [Brt])
                op('dve', L('tensor_tensor', out=O1, in0=a1, in1=a2, op=ALU.subtract), [Brt], [Bd])
                op('dve', L('tensor_tensor', out=a1, in0=X2, in1=Cb, op=ALU.mult), [Bs, Brope, Bd], [Brt])
                op('dve', L('tensor_tensor', out=a2, in0=X1, in1=Sb, op=ALU.mult), [Bs, Brope, Brt], [Brt])
                op('dve', L('tensor_tensor', out=O2, in0=a1, in1=a2, op=ALU.add), [Brt, Bd], [Bd])

            for b in range(nblk):
                nt = TB // 128
                hT, BhT = hTs[b % 2], BhTs[b % 2]
                for tt in range(nt):
                    t = b * nt + tt
                    s = t % 2
                    load(xst[s], x_src[t * 128:(t + 1) * 128, :], Bxst[s])
                    for c in range(8):
                        tr(ps[:, c // 4, (c % 4) * 128:(c % 4 + 1) * 128], xst[s][:, c * 128:(c + 1) * 128], 128, [Bxst[s]], [PB[c // 4]])
                    for c in range(8):
                        for (sq_, t0, n) in segs:
                            lo = max(t0, t * 128)
                            hi = min(t0 + n, (t + 1) * 128)
                            if lo >= hi:
                                continue
                            l0 = lo - t * 128
                            w_ = hi - lo
                            op('dve', L('tensor_scalar',
                                out=hT[:, c, tt * 128 + l0:tt * 128 + l0 + w_],
                                in0=ps[:, c // 4, (c % 4) * 128 + l0:(c % 4) * 128 + l0 + w_],
                                scalar1=modT[:, 8 + c, sq_:sq_ + 1], scalar2=modT[:, c, sq_:sq_ + 1], op0=ALU.mult, op1=ALU.add),
                               [PB[c // 4], Bmod], [BhT])
                for oc in range(4):
                    for k in range(8):
                        mm(ps[:, 2, 0:TB], w_in_bf[:, k, (4 + oc) * 128:(5 + oc) * 128], hT[:, k, :], k == 0, k == 7, [Bwin, BhT], [PB[2]])
                    for k in range(8):
                        mm(ps[:, 3, 0:TB], w_in_bf[:, k, oc * 128:(oc + 1) * 128], hT[:, k, :], k == 0, k == 7, [Bwin, BhT], [PB[3]])
                    op('act', L('activation', out=sig_t[:, 0:TB], in_=ps[:, 2, 0:TB], func=AF.Sigmoid,
                                                            bias=smallT[:, 20 + oc:21 + oc]), [PB[2], Bc], [Bsig])
                    if isP:
                        o_ap = u_ext[:, oc, HIST + b * 512:HIST + (b + 1) * 512]
                        i0 = ps[:, 3, 0:512]
                        i1 = sig_t[:, 0:512]
                    else:
                        o_ap = u_ext[:, oc, 0:248].rearrange("p (s t) -> p s t", s=4)[:, :, HIST:62]
                        i0 = ps[:, 3, 0:128].rearrange("p (s t) -> p s t", s=4)
                        i1 = sig_t[:, 0:128].rearrange("p (s t) -> p s t", s=4)
                    op('dve', L('scalar_tensor_tensor',
                        out=o_ap, in0=i0, scalar=smallT[:, 16 + oc:17 + oc], in1=i1, op0=ALU.add, op1=ALU.mult), [PB[3], Bsig, Bc], [Bu])
                for tt in range(nt):
                    t = b * nt + tt
                    hs = hT[:, :, tt * 128:(tt + 1) * 128]
                    for part, pbk in ((0, 4), (1, 5), (2, 6)):
                        for k in range(8):
                            mm(ps[:, pbk, :], hs[:, k, :], w_in_bf[:, k, 1024 + part * 512:1536 + part * 512], k == 0, k == 7, [Bwin, BhT], [PB[pbk]])
                    op('dve', L('tensor_tensor', out=qb, in0=ps[:, 4, :], in1=binbc[:, 0:512], op=ALU.add), [PB[4], Bbin], [Bqb])
                    op('dve', L('tensor_tensor', out=kb, in0=ps[:, 5, :], in1=binbc[:, 512:1024], op=ALU.add), [PB[5], Bbin], [Bkb])
                    op('dve', L('tensor_tensor', out=vb, in0=ps[:, 6, :], in1=binbc[:, 1024:1536], op=ALU.add), [PB[6], Bbin], [Bvb])
                    rope(qb, qr, Bqb, Bqr, t)
                    rope(kb, kr, Bkb, Bkr, t)
                    store(k_dst[t * 128:(t + 1) * 128, :], kr, Bkr)
                    store(v_dst[t * 128:(t + 1) * 128, :], vb, Bvb)
                    for h in range(4):
                        tr(ps[:, 7, h * 128:(h + 1) * 128], qr[:, h * 128:(h + 1) * 128], 128, [Bqr], [PB[7]])
                    op('act', L('copy', out=qT[:, :, t * 128:(t + 1) * 128], in_=ps[:, 7, :].rearrange("p (h n) -> p h n", h=4)), [PB[7]], [BqT])
                    for h in range(4):
                        tr(ps[:, 7, h * 128:(h + 1) * 128], kr[:, h * 128:(h + 1) * 128], 128, [Bkr], [PB[7]])
                    if isP:
                        op('act', L('copy', out=kT[:, :, t * 128:(t + 1) * 128], in_=ps[:, 7, :].rearrange("p (h n) -> p h n", h=4)), [PB[7]], [BkT])
                        op('act', L('copy', out=Vp[:, t, :, 0:128], in_=vb.rearrange("p (h n) -> p h n", h=4)), [Bvb], [BVp])
                    else:
                        op('act', L('copy', out=kTn, in_=ps[:, 7, :].rearrange("p (h n) -> p h n", h=4)), [PB[7]], [BkTn])
                        for j in range(4):
                            for k in range(8):
                                mm(ps[0:32, 6, :], hT[:, k, 32 * j:32 * j + 32], w_in_bf[:, k, 2048:2560], k == 0, k == 7, [Bwin, BhT], [PB[6]])
                            op('dve', L('tensor_tensor', out=Vn[:, j, :, 0:128], in0=ps[0:32, 6, :].rearrange("p (h n) -> p h n", h=4),
                                                                     in1=binbc[0:32, 1024:1536].rearrange("p (h n) -> p h n", h=4), op=ALU.add),
                               [PB[6], Bbin], [BVn])
            cst = A.alloc([HIST, 512])
            Bcst = Buf("cst")
            for (sq_, t0, n) in segs:
                if isP:
                    c0 = SEQ
                    dst = conv_p[U['idx'] * HIST:(U['idx'] + 1) * HIST, :]
                else:
                    j = sq_ - 2
                    c0 = 62 * j + 32
                    dst = conv_s[j * HIST:(j + 1) * HIST, :]
                for c in range(4):
                    tr(ps[0:HIST, 0, c * 128:(c + 1) * 128], u_ext[:, c, c0:c0 + HIST], 128, [Bu], [PB[0]])
                op('dve', L('tensor_copy', out=cst, in_=ps[0:HIST, 0, :]), [PB[0]], [Bcst])
                store(dst, cst, Bcst)
            T.barrier()
            A.release()
            if STAGE < 2:
                A.release()
                continue
            mixT = A.alloc([128, 8, SEQ], BF16)

            A.mark()
            NB = TB
            dw = A.alloc([128, 4, NB])
            sq = A.alloc([128, 4, NB])
            mean_sb = A.alloc([128, NB]); m2 = A.alloc([128, NB]); rstd = A.alloc([128, NB])
            Bdw = [Buf("dw%d" % c) for c in range(4)]
            Bsq, Bst = Buf("sq"), Buf("stat")

            def v3(ap):
                return ap if isP else ap.rearrange("p (s t) -> p s t", s=4)
            if isP and CONV_PE:
                u_bf = A.alloc([128, 4, HIST + SEQ], BF16)
                diagw = A.alloc([128, 4, CW, 128], BF16)
                Bubf, Bdiag = Buf("u_bf"), Buf("diagw")
                for c in range(4):
                    op('act', L('copy', out=u_bf[:, c, :], in_=u_ext[:, c, :]), [Bu], [Bubf])
                    op('dve', L('tensor_tensor', out=diagw[:, c, :, :], in0=ident.unsqueeze(1).broadcast_to([128, CW, 128]),
                                in1=wdwT[:, c, :].unsqueeze(2).broadcast_to([128, CW, 128]), op=ALU.mult), [Bc], [Bdiag])
            crot = 0
            for b in range(nblk):
                for c in range(4):
                    if isP and CONV_PE:
                        bank = 2 + crot % 4
                        crot += 1
                        for j in range(CW):
                            mm(ps[:, bank, 0:512], diagw[:, c, j, :], u_bf[:, c, b * 512 + j:b * 512 + j + 512], j == 0, j == CW - 1,
                               [Bdiag, Bubf], [PB[bank]])
                        op('act', L('activation', out=dw[:, c, :], in_=ps[:, bank, 0:512], func=AF.Identity, bias=smallT[:, 24 + c:25 + c]),
                           [PB[bank], Bc], [Bdw[c]])
                        continue
                    eng = 'dve'
                    o_ap = v3(dw[:, c, :])
                    op(eng, L('tensor_scalar', out=o_ap, in0=ucols(c, 0, NB, b), scalar1=wdwT[:, c, 0:1],
                                                                           scalar2=smallT[:, 24 + c:25 + c], op0=ALU.mult, op1=ALU.add),
                       [Bu, Bc], [Bdw[c]])
                    for j in range(1, CW):
                        op(eng, L('scalar_tensor_tensor', out=o_ap, in0=ucols(c, j, NB, b), scalar=wdwT[:, c, j:j + 1],
                                                                                           in1=o_ap, op0=ALU.mult, op1=ALU.add), [Bu, Bc, Bdw[c]], [Bdw[c]])
                op('act', L('activation', out=sq, in_=dw, func=AF.Square), Bdw, [Bsq])
                for c in range(4):
                    mm(ps[:, 0, 0:NB], onesdiv, dw[:, c, :], c == 0, c == 3, [Bc] + Bdw, [PB[0]])
                for c in range(4):
                    mm(ps[:, 1, 0:NB], onesdiv, sq[:, c, :], c == 0, c == 3, [Bc, Bsq], [PB[1]])
                op('dve', L('tensor_copy', out=mean_sb, in_=ps[:, 0, 0:NB]), [PB[0]], [Bst])
                op('dve', L('tensor_tensor', out=m2, in0=mean_sb, in1=mean_sb, op=ALU.mult), [Bst], [Bst])
                op('dve', L('tensor_tensor', out=m2, in0=ps[:, 1, 0:NB], in1=m2, op=ALU.subtract), [PB[1], Bst], [Bst])
                op('act', L('activation', out=rstd, in_=m2, func=AF.Sqrt, bias=EPS, scale=1.0), [Bst], [Bst])
                op('dve', L('reciprocal', out=rstd, in_=rstd), [Bst], [Bst])
                for c in range(4):
                    eng = 'dve' if c % 2 == 0 else 'pool'
                    op(eng, L('tensor_tensor', out=dw[:, c, :], in0=dw[:, c, :], in1=mean_sb, op=ALU.subtract), [Bst, Bdw[c]], [Bdw[c]])
                    op(eng, L('tensor_tensor', out=dw[:, c, :], in0=dw[:, c, :], in1=rstd, op=ALU.mult), [Bst, Bdw[c]], [Bdw[c]])
                    op('act', L('activation', out=mixT[:, c, b * NB:(b + 1) * NB], in_=dw[:, c, :], func=AF.Silu,
                                                               bias=smallT[:, 32 + c:33 + c], scale=smallT[:, 28 + c:29 + c]), [Bdw[c], Bc], [Bmix])
            T.barrier()
            A.release()
            if STAGE < 3:
                A.release()
                continue

            A.mark()
            o1n = A.alloc([128, 128]); oc_ = A.alloc([128, 128]); junk = A.alloc([128, 128])
            sm = A.alloc([128, 8])
            Bo = Buf("o_work")
            att = A.alloc([128, 4, 512])
            Batt = Buf("att")

            def combine(np_, pb1, pb2, att_ap):
                P = slice(0, np_)
                op('dve', L('reciprocal', out=sm[P, 0:1], in_=ps[P, pb1, 128:129]), [PB[pb1]], [Bo])
                op('dve', L('reciprocal', out=sm[P, 1:2], in_=ps[P, pb2, 128:129]), [PB[pb2], Bo], [Bo])
                op('dve', L('tensor_tensor', out=sm[P, 1:2], in0=sm[P, 1:2], in1=lam_t[P, 1:2], op=ALU.mult), [Bo, Bc], [Bo])
                op('dve', L('tensor_scalar', out=o1n[P, :], in0=ps[P, pb1, 0:128], scalar1=sm[P, 0:1], scalar2=None, op0=ALU.mult), [PB[pb1], Bo], [Bo])
                op('dve', L('scalar_tensor_tensor', out=oc_[P, :], in0=ps[P, pb2, 0:128], scalar=sm[P, 1:2], in1=o1n[P, :],
                                                           op0=ALU.mult, op1=ALU.add), [PB[pb2], Bo], [Bo])
                op('dve', L('memset', sm[P, 2:3], 0.0), [Bo], [Bo])
                op('dve', L('scalar_tensor_tensor', out=junk[P, :], in0=oc_[P, :], scalar=1.0, in1=oc_[P, :], op0=ALU.mult, op1=ALU.mult,
                                                           accum_out=sm[P, 2:3]), [Bo], [Bo])
                op('act', L('activation', out=sm[P, 3:4], in_=sm[P, 2:3], func=AF.Sqrt, bias=EPS, scale=1.0 / 128.0), [Bo], [Bo])
                op('dve', L('reciprocal', out=sm[P, 4:5], in_=sm[P, 3:4]), [Bo], [Bo])
                op('dve', L('scalar_tensor_tensor', out=att_ap, in0=oc_[P, :], scalar=sm[P, 4:5], in1=gsub[P, :], op0=ALU.mult, op1=ALU.mult),
                   [Bo, Bc], [Batt])

            ssb = A.alloc([128, 16]); rsb = A.alloc([128, 16])
            Bss = Buf("ssb")

            def combine_def(pb1, pb2, att_ap, col):
                op('dve', L('reciprocal', out=sm[:, 0:1], in_=ps[:, pb1, 128:129]), [PB[pb1]], [Bo])
                op('dve', L('reciprocal', out=sm[:, 1:2], in_=ps[:, pb2, 128:129]), [PB[pb2], Bo], [Bo])
                op('dve', L('tensor_tensor', out=sm[:, 1:2], in0=sm[:, 1:2], in1=lam_t[:, 1:2], op=ALU.mult), [Bo, Bc], [Bo])
                op('dve', L('tensor_scalar', out=o1n, in0=ps[:, pb1, 0:128], scalar1=sm[:, 0:1], scalar2=None, op0=ALU.mult), [PB[pb1], Bo], [Bo])
                op('dve', L('scalar_tensor_tensor', out=att_ap, in0=ps[:, pb2, 0:128], scalar=sm[:, 1:2], in1=o1n,
                            op0=ALU.mult, op1=ALU.add), [PB[pb2], Bo], [Batt])
                op('dve', L('scalar_tensor_tensor', out=junk, in0=att_ap, scalar=1.0, in1=att_ap, op0=ALU.mult, op1=ALU.mult,
                            accum_out=ssb[:, col:col + 1]), [Batt, Bss], [Bo, Bss])

            def finish_block():
                op('act', L('activation', out=rsb, in_=ssb, func=AF.Sqrt, bias=EPS, scale=1.0 / 128.0), [Bss], [Bss])
                op('dve', L('reciprocal', out=rsb, in_=rsb), [Bss], [Bss])
                for qq in range(4):
                    for hh in range(4):
                        a_ap = att[:, qq, hh * 128:(hh + 1) * 128]
                        op('dve', L('scalar_tensor_tensor', out=a_ap, in0=a_ap, scalar=rsb[:, qq * 4 + hh:qq * 4 + hh + 1], in1=gsub,
                                    op0=ALU.mult, op1=ALU.mult), [Bss, Bc, Batt], [Batt])

            if isP:
                PTs_ = [[A.alloc([128, 16, 512], BF16) for _ in range(2)] for _ in range(2)]
                BPTs_ = [[Buf("PT%d%d" % (a_, b_)) for b_ in range(2)] for a_ in range(2)]
                rot = [0]
                items = [(j, h) for j in range(4) for h in range(4)]

                def qk_exp(n):
                    j, h = items[n]
                    PT, BPT = PTs_[n % 2], BPTs_[n % 2]
                    nkt = 4 * (j + 1)
                    for p in range(2):
                        rows = slice(p * 64, (p + 1) * 64)
                        for kt in range(nkt):
                            off = max(0, kt - 4 * j) * 128
                            N = 512 - off
                            pb = rot[0] % 4
                            rot[0] += 1
                            mm(ps[:, pb, 0:N], kT[rows, h, kt * 128:(kt + 1) * 128], qT[rows, h, j * 512 + off:(j + 1) * 512], True, True,
                               [BkT, BqT], [PB[pb]])
                            op('act', L('activation', out=PT[p][:, kt, off:512], in_=ps[:, pb, 0:N], func=AF.Exp, scale=0.125), [PB[pb]], [BPT[p]])
                            if kt >= 4 * j:
                                op('pool', L('memset', PT[p][64:128, kt, off:off + 64], 0.0), (), [BPT[p]])

                def pv_combine(n):
                    j, h = items[n]
                    PT, BPT = PTs_[n % 2], BPTs_[n % 2]
                    for qq in range(4):
                        qt = 4 * j + qq
                        pbs = (4 + 2 * (qq % 2), 5 + 2 * (qq % 2))
                        for p in range(2):
                            for kt in range(qt + 1):
                                mm(ps[:, pbs[p], 0:130], PT[p][:, kt, qq * 128:(qq + 1) * 128], Vp[:, kt, h, :], kt == 0, kt == qt,
                                   [BPT[p], BVp], [PB[pbs[p]]])
                        combine_def(pbs[0], pbs[1], att[:, qq, h * 128:(h + 1) * 128], qq * 4 + h)

                qk_exp(0)
                for n in range(16):
                    j, h = items[n]
                    if n + 1 < 16:
                        qk_exp(n + 1)
                    if h == 0:
                        op('dve', L('memset', ssb, 0.0), [Bss], [Bss])
                    pv_combine(n)
                    if h == 3:
                        finish_block()
                        for qq in range(4):
                            qt = 4 * j + qq
                            pb = qq % 4
                            for hh in range(4):
                                tr(ps[:, pb, hh * 128:(hh + 1) * 128], att[:, qq, hh * 128:(hh + 1) * 128], 128, [Batt], [PB[pb]])
                            op('act', L('copy', out=mixT[:, 4:8, qt * 128:(qt + 1) * 128],
                                        in_=ps[:, pb, :].rearrange("p (h n) -> p h n", h=4)), [PB[pb]], [Bmix])
            else:
                PTs = [A.alloc([128, 16, 32], BF16) for _ in range(2)]
                PTn = A.alloc([32, 2, 32], BF16)
                BPT = [Buf("PTs0"), Buf("PTs1")]
                BPTn = Buf("PTn")
                kst = [A.alloc([128, 512]) for _ in range(4)]
                vst = [A.alloc([128, 512]) for _ in range(4)]
                Bkst = [Buf("kst%d" % i) for i in range(4)]
                Bvst = [Buf("vst%d" % i) for i in range(4)]
                for j in range(4):
                    for kt in range(16):
                        s = kt % 4
                        load(kst[s], ck[j * PAST + kt * 128:j * PAST + (kt + 1) * 128, :], Bkst[s])
                        load(vst[s], cv[j * PAST + kt * 128:j * PAST + (kt + 1) * 128, :], Bvst[s])
                        pb = kt % 2
                        for h in range(4):
                            tr(ps[:, pb, h * 128:(h + 1) * 128], kst[s][:, h * 128:(h + 1) * 128], 128, [Bkst[s]], [PB[pb]])
                        op('act', L('copy', out=kT[:, :, kt * 128:(kt + 1) * 128], in_=ps[:, pb, :].rearrange("p (h n) -> p h n", h=4)),
                           [PB[pb]], [BkT])
                        op('pool', L('tensor_copy', out=Vp[:, kt, :, 0:128], in_=vst[s].rearrange("p (h n) -> p h n", h=4)), [Bvst[s]], [BVp])
                    qc = slice(32 * j, 32 * j + 32)
                    for h in range(4):
                        for p in range(2):
                            rows = slice(p * 64, (p + 1) * 64)
                            pb = 2 + p
                            for kt in range(16):
                                mm(ps[:, pb, kt * 32:(kt + 1) * 32], kT[rows, h, kt * 128:(kt + 1) * 128], qT[rows, h, qc], True, True, [BkT, BqT], [PB[pb]])
                            op('act', L('activation', out=PTs[p], in_=ps[:, pb, :].rearrange("p (k q) -> p k q", k=16), func=AF.Exp, scale=0.125),
                               [PB[pb]], [BPT[p]])
                            mm(ps[0:32, 4, p * 32:(p + 1) * 32], kTn[rows, h, qc], qT[rows, h, qc], True, True, [BkTn, BqT], [PB[4]])
                        op('act', L('activation', out=PTn, in_=ps[0:32, 4, 0:64].rearrange("p (a q) -> p a q", a=2), func=AF.Exp, scale=0.125), [PB[4]], [BPTn])
                        for p in range(2):
                            pb = 5 + p
                            for kt in range(16):
                                mm(ps[0:32, pb, 0:130], PTs[p][:, kt, :], Vp[:, kt, h, :], kt == 0, False, [BPT[p], BVp], [PB[pb]])
                            mm(ps[0:32, pb, 0:130], PTn[:, p, :], Vn[:, j, h, :], False, True, [BPTn, BVn], [PB[pb]])
                        combine(32, 5, 6, att[0:32, 0, h * 128:(h + 1) * 128])
                    for h in range(4):
                        tr(ps[:, 7, h * 32:(h + 1) * 32], att[0:32, 0, h * 128:(h + 1) * 128], 32, [Batt], [PB[7]])
                    op('act', L('copy', out=mixT[:, 4:8, 32 * j:32 * j + 32], in_=ps[:, 7, 0:128].rearrange("p (h n) -> p h n", h=4)), [PB[7]], [Bmix])
            T.barrier()
            A.release()
            if STAGE < 4:
                A.release()
                continue

            A.mark()
            w_out_bf = A.alloc([128, 8, D], BF16)
            Bwo = Buf("w_out_bf")
            wst = [A.alloc([128, 8, 256]) for _ in range(2)]
            Bwst = [Buf("wst0"), Buf("wst1")]
            w_out_v = w_out.rearrange("(k p) n -> p k n", p=128)
            for cb in range(4):
                s = cb % 2
                load(wst[s], w_out_v[:, :, cb * 256:(cb + 1) * 256], Bwst[s])
                op('dve' if s == 0 else 'pool', L('tensor_copy', out=w_out_bf[:, :, cb * 256:(cb + 1) * 256], in_=wst[s]), [Bwst[s]], [Bwo])
            bout_bc = A.alloc([128, D]); l1g = A.alloc([128, D]); l1b = A.alloc([128, D]); g1bc = A.alloc([128, D])
            Bbc = Buf("bc1")
            load(bout_bc, b_out[0:1, :].partition_broadcast(128), Bbc)
            load(l1g, ln1_g[0:1, :].partition_broadcast(128), Bbc)
            load(l1b, ln1_b[0:1, :].partition_broadcast(128), Bbc)
            for (sq_, t0, n) in segs:
                npart = 128 if isP else 32
                p0 = 0 if isP else t0
                T.dma('sp', L('dma_start', out=g1bc[p0:p0 + npart, :],
                                                                               in_=modscr[sq_:sq_ + 1, 0:1024].partition_broadcast(npart)), [Bscr], [Bbc])
            xst = [A.alloc([128, D]) for _ in range(2)]
            Bxst = [Buf("xst0"), Buf("xst1")]
            tws = [A.alloc([128, D]) for _ in range(2)]; rws = [A.alloc([128, D]) for _ in range(2)]
            x1w = [A.alloc([128, D]) for _ in range(2)]
            sm1s = [A.alloc([128, 8]) for _ in range(2)]
            Bts, Brs, Bx1w, Bsms = [Buf("tw0"), Buf("tw1")], [Buf("rw0"), Buf("rw1")], [Buf("x1w0"), Buf("x1w1")], [Buf("sm10"), Buf("sm11")]
            tw, rw, sm1, Bt, Br, Bsm = tws[0], rws[0], sm1s[0], Bts[0], Brs[0], Bsms[0]

            def layer_norm_tail(rw_, Br_, s1col, out_ap, Bout, gbc, bbc, Bg):
                return ln_tail(sm1, Bsm, tw, Bt, rw_, Br_, out_ap, Bout, gbc, bbc, Bg)

            def _unused(rw_, Br_, s1col, out_ap, Bout, gbc, bbc, Bg):
                op('dve', L('memset', sm1[:, 1:2], 0.0), [Bsm], [Bsm])
                op('dve', L('scalar_tensor_tensor', out=tw, in0=rw_, scalar=1.0, in1=rw_, op0=ALU.mult, op1=ALU.mult, accum_out=sm1[:, 1:2]),
                   [Br_, Bsm], [Bt, Bsm])
                op('dve', L('tensor_scalar', out=sm1[:, 2:3], in0=sm1[:, s1col:s1col + 1], scalar1=-1.0 / D, scalar2=None, op0=ALU.mult), [Bsm], [Bsm])
                op('dve', L('tensor_tensor', out=sm1[:, 3:4], in0=sm1[:, 2:3], in1=sm1[:, 2:3], op=ALU.mult), [Bsm], [Bsm])
                op('dve', L('scalar_tensor_tensor', out=sm1[:, 4:5], in0=sm1[:, 1:2], scalar=1.0 / D, in1=sm1[:, 3:4], op0=ALU.mult, op1=ALU.subtract),
                   [Bsm], [Bsm])
                op('act', L('activation', out=sm1[:, 5:6], in_=sm1[:, 4:5], func=AF.Sqrt, bias=EPS, scale=1.0), [Bsm], [Bsm])
                op('dve', L('reciprocal', out=sm1[:, 6:7], in_=sm1[:, 5:6]), [Bsm], [Bsm])
                op('dve', L('tensor_scalar', out=tw, in0=rw_, scalar1=sm1[:, 2:3], scalar2=sm1[:, 6:7], op0=ALU.add, op1=ALU.mult), [Br_, Bsm, Bt], [Bt])
                op('pool', L('tensor_tensor', out=tw, in0=tw, in1=gbc, op=ALU.mult), [Bt, Bg], [Bt])
                op('pool', L('tensor_tensor', out=out_ap, in0=tw, in1=bbc, op=ALU.add), [Bt, Bg], [Bout])

            for t in range(ntile):
                s = t % 2
                tw, rw, sm1, Bt, Br, Bsm = tws[s], rws[s], sm1s[s], Bts[s], Brs[s], Bsms[s]
                load(xst[s], x_src[t * 128:(t + 1) * 128, :], Bxst[s])
                for cb in range(2):
                    for k in range(8):
                        mm(ps[:, cb, :], mixT[:, k, t * 128:(t + 1) * 128], w_out_bf[:, k, cb * 512:(cb + 1) * 512], k == 0, k == 7, [Bmix, Bwo], [PB[cb]])
                op('dve', L('tensor_tensor', out=tw.rearrange("p (a b) -> p a b", a=2), in0=ps[:, 0:2, :],
                                                    in1=bout_bc.rearrange("p (a b) -> p a b", a=2), op=ALU.add), [PB[0], PB[1], Bbc], [Bt])
                op('dve', L('tensor_tensor', out=tw, in0=tw, in1=g1bc, op=ALU.mult), [Bt, Bbc], [Bt])
                op('dve', L('memset', sm1[:, 0:1], 0.0), [Bsm], [Bsm])
                op('dve', L('scalar_tensor_tensor', out=rw, in0=xst[s], scalar=float(ALPHA), in1=tw, op0=ALU.mult, op1=ALU.add,
                                                                accum_out=sm1[:, 0:1]), [Bxst[s], Bt, Bsm], [Br, Bsm])
                ln_tail(sm1, Bsm, tw, Bt, rw, Br, x1w[s], Bx1w[s], l1g, l1b, Bbc)
                store(x1scr[(tile0 + t) * 128:(tile0 + t + 1) * 128, :], x1w[s], Bx1w[s], [Bx1[tile0 + t]])
            T.barrier()
            A.release()
            A.release()

        if STAGE >= 5:
            T.barrier()
            A.top = top_consts
            A.mark()
            wq_bf = A.alloc([128, 8, 2048], BF16)
            Bwq = Buf("wq_bf")
            skT = A.alloc([128, 16, 128], BF16)
            BskT = Buf("skT")
            A.mark()
            wst = [A.alloc([128, 8, 256]) for _ in range(2)]
            Bwst = [Buf("wst0"), Buf("wst1")]
            w_q_v = w_query.rearrange("(k p) n -> p k n", p=128)
            for cb in range(8):
                s = cb % 2
                load(wst[s], w_q_v[:, :, cb * 256:(cb + 1) * 256], Bwst[s])
                op('dve' if s == 0 else 'pool', L('tensor_copy', out=wq_bf[:, :, cb * 256:(cb + 1) * 256], in_=wst[s]), [Bwst[s]], [Bwq])
            skst = [A.alloc([128, 128]) for _ in range(2)]
            Bskst = [Buf("sk0"), Buf("sk1")]
            for hp in range(16):
                s = hp % 2
                load(skst[s], sub_keys[hp * 128:(hp + 1) * 128, :], Bskst[s])
                tr(ps[:, s, 0:128], skst[s], 128, [Bskst[s]], [PB[s]])
                op('act', L('copy', out=skT[:, hp, :], in_=ps[:, s, 0:128]), [PB[s]], [BskT])
            T.barrier()
            A.release()
            bcs = [[A.alloc([128, D]) for _ in range(3)] for _ in range(2)]
            Bbcs = [Buf("bc2a"), Buf("bc2b")]
            l2g = A.alloc([128, D]); l2b = A.alloc([128, D])
            Bl2 = Buf("l2")
            load(l2g, ln2_g[0:1, :].partition_broadcast(128), Bl2)
            load(l2b, ln2_b[0:1, :].partition_broadcast(128), Bl2)
            x1t = [A.alloc([128, D]) for _ in range(2)]
            Bx1t = [Buf("x1t0"), Buf("x1t1")]
            h2s = [A.alloc([128, D]) for _ in range(2)]; Bh2s = [Buf("h2a"), Buf("h2b")]
            h2T = A.alloc([128, D], BF16); Bh2T = Buf("h2T")
            qpT = A.alloc([128, 16, 128], BF16); BqpT = Buf("qpT")
            s_sb = A.alloc([128, 2048]); Bs = Buf("s_sb")
            rep = A.alloc([128, 256]); Brep = Buf("rep")
            sv = A.alloc([128, 16, 16]); si = A.alloc([128, 16, 16], U32); sif = A.alloc([128, 16, 16])
            Bsv, Bsi = Buf("sv"), Buf("si")
            cand = A.alloc([128, 8, 16, 16]); Bcand = Buf("cand")
            fv = A.alloc([128, 8, 16]); fp_ = A.alloc([128, 8, 16], U32); Bfv = Buf("fv")
            fiu = A.alloc([128, 8, 16], U32); fju = A.alloc([128, 8, 16], U32); fif = A.alloc([128, 8, 16]); fjf = A.alloc([128, 8, 16])
            Bfi = Buf("fi")
            oh = A.alloc([128, 8, 16, 16]); Boh = Buf("oh")
            e0 = A.alloc([128, 8, 16]); e1 = A.alloc([128, 8, 16]); ef = A.alloc([128, 128])
            eidxs = [A.alloc([128, 128], I32) for _ in range(2)]
            Be, Beidxs = Buf("e"), [Buf("eidxa"), Buf("eidxb")]
            gws = [A.alloc([128, 8, 16]) for _ in range(2)]; gs = A.alloc([128, 8]); Bgs = [Buf("gwa"), Buf("gwb")]
            av = A.alloc([128, 128]); wgt = A.alloc([128, 128]); Ba, Bwg = Buf("a"), Buf("wgt")
            junk2 = A.alloc([128, D]); Bj2 = Buf("junk2")
            acc = A.alloc([128, D]); Bacc = Buf("acc")
            tw = A.alloc([128, D]); rw = A.alloc([128, D]); yw = [A.alloc([128, D]) for _ in range(2)]
            sm1 = A.alloc([128, 8])
            Bt, Br, Byw, Bsm = Buf("tw"), Buf("rw"), [Buf("yw0"), Buf("yw1")], Buf("sm1")
            NG = min(NRING, (A.n - A.top) // 1024)
            print("PEER gather ring buffers:", NG, flush=True)
            ring = [A.alloc([128, D]) for _ in range(NG)]
            Bring = [Buf("ring%d" % i) for i in range(NG)]
            ring_state = [0]

            def bc_iota():
                return iota16.unsqueeze(1).unsqueeze(1).broadcast_to([128, 8, 16, 16])

            tiles = [(t, 'p', t // 16) for t in range(32)] + [(32, 's', 0)]
            if STAGE < 9:
                tiles = [(t, 'p', 0) for t in range(16)] + [(32, 's', 0)]
            seq_set = {}
            for (tg, kind, sidx) in tiles:
                if (kind, sidx) not in seq_set:
                    seq_set[(kind, sidx)] = len(seq_set) % 2
            loaded = set()

            def front(ti):
                tg, kind, sidx = tiles[ti]
                s = ti % 2
                bset = seq_set[(kind, sidx)]
                sc2bc, sh2bc, g2bc = bcs[bset]
                Bbc2 = Bbcs[bset]
                h2, Bh2 = h2s[s], Bh2s[s]
                eidx, Beidx = eidxs[s], Beidxs[s]
                gw, Bg = gws[s], Bgs[s]
                if (kind, sidx) not in loaded:
                    loaded.add((kind, sidx))
                    if kind == 'p':
                        rows = [(sidx, 0, 128)]
                    else:
                        rows = [(2 + j, 32 * j, 32) for j in range(4)]
                    for (sq_, p0, npart) in rows:
                        T.dma('sp', L('dma_start', out=sh2bc[p0:p0 + npart, :], in_=modscr[sq_:sq_ + 1, 1024:2048].partition_broadcast(npart)), [Bscr], [Bbc2])
                        T.dma('sp', L('dma_start', out=sc2bc[p0:p0 + npart, :], in_=modscr[sq_:sq_ + 1, 2048:3072].partition_broadcast(npart)), [Bscr], [Bbc2])
                        T.dma('sp', L('dma_start', out=g2bc[p0:p0 + npart, :], in_=modscr[sq_:sq_ + 1, 3072:4096].partition_broadcast(npart)), [Bscr], [Bbc2])
                T.dma('sp', L('dma_start', out=x1t[s], in_=x1scr[tg * 128:(tg + 1) * 128, :]), [Bx1[tg]], [Bx1t[s]])
                op('dve', L('tensor_tensor', out=h2, in0=x1t[s], in1=sc2bc, op=ALU.mult), [Bx1t[s], Bbc2], [Bh2])
                op('dve', L('tensor_tensor', out=h2, in0=h2, in1=sh2bc, op=ALU.add), [Bh2, Bbc2], [Bh2])
                yield
                for c in range(8):
                    tr(ps[:, c // 4, (c % 4) * 128:(c % 4 + 1) * 128], h2[:, c * 128:(c + 1) * 128], 128, [Bh2], [PB[c // 4]])
                op('act', L('copy', out=h2T.rearrange("p (a b) -> p a b", a=2), in_=ps[:, 0:2, :]), [PB[0], PB[1]], [Bh2T])
                for g4 in range(4):
                    bank = 2 + g4 % 2
                    for i in range(4):
                        hp = g4 * 4 + i
                        for k in range(8):
                            mm(ps[:, bank, i * 128:(i + 1) * 128], wq_bf[:, k, hp * 128:(hp + 1) * 128], h2T[:, k * 128:(k + 1) * 128], k == 0, k == 7,
                               [Bwq, Bh2T], [PB[bank]])
                    op('act', L('copy', out=qpT[:, g4 * 4:(g4 + 1) * 4, :], in_=ps[:, bank, :].rearrange("p (a b) -> p a b", a=4)), [PB[bank]], [BqpT])
                yield
                for g4 in range(4):
                    bank = 4 + g4 % 2
                    for i in range(4):
                        hp = g4 * 4 + i
                        mm(ps[:, bank, i * 128:(i + 1) * 128], qpT[:, hp, :], skT[:, hp, :], True, True, [BqpT, BskT], [PB[bank]])
                    op('act', L('copy', out=s_sb[:, g4 * 512:(g4 + 1) * 512], in_=ps[:, bank, :]), [PB[bank]], [Bs])
                yield
                for hp in range(16):
                    sl = s_sb[:, hp * 128:(hp + 1) * 128]
                    op('dve', L('max', out=sv[:, hp, 0:8], in_=sl), [Bs], [Bsv])
                    op('dve', L('match_replace', out=rep[:, 0:128], in_to_replace=sv[:, hp, 0:8], in_values=sl, imm_value=-1e30), [Bs, Bsv], [Brep])
                    op('dve', L('max', out=sv[:, hp, 8:16], in_=rep[:, 0:128]), [Brep], [Bsv])
                    op('dve', L('max_index', out=si[:, hp, 0:8], in_max=sv[:, hp, 0:8], in_values=sl), [Bs, Bsv], [Bsi])
                    op('dve', L('max_index', out=si[:, hp, 8:16], in_max=sv[:, hp, 8:16], in_values=sl), [Bs, Bsv], [Bsi])
                    yield
                op('dve', L('tensor_copy', out=sif, in_=si), [Bsi], [Bsi])
                svv = sv.rearrange("p (h two) i -> p h two i", two=2)
                sfv = sif.rearrange("p (h two) i -> p h two i", two=2)
                op('pool', L('tensor_tensor', out=cand, in0=svv[:, :, 0, :].unsqueeze(3).broadcast_to([128, 8, 16, 16]),
                             in1=svv[:, :, 1, :].unsqueeze(2).broadcast_to([128, 8, 16, 16]), op=ALU.add), [Bsv], [Bcand])
                yield
                for h in range(8):
                    ch_ = cand[:, h, :, :].rearrange("p a b -> p (a b)")
                    op('dve', L('max', out=fv[:, h, 0:8], in_=ch_), [Bcand], [Bfv])
                    op('dve', L('match_replace', out=rep, in_to_replace=fv[:, h, 0:8], in_values=ch_, imm_value=-1e30), [Bcand, Bfv], [Brep])
                    op('dve', L('max', out=fv[:, h, 8:16], in_=rep), [Brep], [Bfv])
                    op('dve', L('max_index', out=fp_[:, h, 0:8], in_max=fv[:, h, 0:8], in_values=ch_), [Bcand, Bfv], [Bfv])
                    op('dve', L('max_index', out=fp_[:, h, 8:16], in_max=fv[:, h, 8:16], in_values=ch_), [Bcand, Bfv], [Bfv])
                    yield
                op('dve', L('tensor_single_scalar', out=fiu, in_=fp_, scalar=4, op=ALU.logical_shift_right), [Bfv], [Bfi])
                op('dve', L('tensor_single_scalar', out=fju, in_=fp_, scalar=15, op=ALU.bitwise_and), [Bfv], [Bfi])
                op('dve', L('tensor_copy', out=fif, in_=fiu), [Bfi], [Bfi])
                op('dve', L('tensor_copy', out=fjf, in_=fju), [Bfi], [Bfi])
                yield
                for (ff_, col, eo) in ((fif, 0, e0), (fjf, 1, e1)):
                    op('dve', L('tensor_tensor', out=oh, in0=bc_iota(), in1=ff_.unsqueeze(3).broadcast_to([128, 8, 16, 16]), op=ALU.is_equal), [Bfi, Bc], [Boh])
                    op('dve', L('tensor_tensor', out=oh, in0=oh, in1=sfv[:, :, col, :].unsqueeze(2).broadcast_to([128, 8, 16, 16]), op=ALU.mult), [Boh, Bsi], [Boh])
                    op('dve', L('tensor_reduce', out=eo, in_=oh, axis=AX.X, op=ALU.add), [Boh], [Be])
                    yield
                op('dve', L('scalar_tensor_tensor', out=ef.rearrange("p (h k) -> p h k", h=8), in0=e0, scalar=128.0, in1=e1, op0=ALU.mult, op1=ALU.add), [Be], [Be])
                op('dve', L('tensor_copy', out=eidx, in_=ef), [Be], [Beidx])
                op('dve', L('tensor_tensor', out=gw, in0=fv, in1=fv[:, :, 0:1].broadcast_to([128, 8, 16]), op=ALU.subtract), [Bfv], [Bg])
                op('act', L('activation', out=gw, in_=gw, func=AF.Exp), [Bg], [Bg])
                op('dve', L('tensor_reduce', out=gs, in_=gw, axis=AX.X, op=ALU.add), [Bg], [Bg])
                op('dve', L('reciprocal', out=gs, in_=gs), [Bg], [Bg])
                op('dve', L('tensor_tensor', out=gw, in0=gw, in1=gs.unsqueeze(2).broadcast_to([128, 8, 16]), op=ALU.mult), [Bg], [Bg])
                yield

            def gather(tab, eidx, Beidx, slot):
                ri = ring_state[0] % NG
                ring_state[0] += 1
                T.dma('pool', L('indirect_dma_start', out=ring[ri], out_offset=None, in_=tab,
                                in_offset=bass.IndirectOffsetOnAxis(ap=eidx[:, slot:slot + 1], axis=0)), [Beidx], [Bring[ri]])
                return ri

            g0 = front(0)
            for _ in g0:
                pass
            for ti in range(len(tiles)):
                tg, kind, sidx = tiles[ti]
                s = ti % 2
                bset = seq_set[(kind, sidx)]
                g2bc, Bbc2 = bcs[bset][2], Bbcs[bset]
                h2, Bh2 = h2s[s], Bh2s[s]
                eidx, Beidx = eidxs[s], Beidxs[s]
                gw, Bg = gws[s], Bgs[s]
                nxt = front(ti + 1) if ti + 1 < len(tiles) else None
                if PEER_ON:
                    op('dve', L('memset', av, 0.0), [Ba], [Ba])
                    for slot in range(128):
                        ri = gather(u_tab, eidx, Beidx, slot)
                        op('dve', L('scalar_tensor_tensor', out=junk2, in0=ring[ri], scalar=1.0, in1=h2, op0=ALU.mult, op1=ALU.mult,
                                    accum_out=av[:, slot:slot + 1]), [Bring[ri], Bh2, Ba], [Bj2, Ba])
                    op('act', L('activation', out=wgt, in_=av, func=AF.Gelu), [Ba], [Bwg])
                    op('dve', L('tensor_tensor', out=wgt, in0=wgt, in1=gw.rearrange("p h k -> p (h k)"), op=ALU.mult), [Bwg, Bg], [Bwg])
                    for slot in range(128):
                        ri = gather(v_tab, eidx, Beidx, slot)
                        if slot == 0:
                            op('dve', L('tensor_scalar', out=acc, in0=ring[ri], scalar1=wgt[:, 0:1], scalar2=None, op0=ALU.mult), [Bring[ri], Bwg], [Bacc])
                        else:
                            op('dve', L('scalar_tensor_tensor', out=acc, in0=ring[ri], scalar=wgt[:, slot:slot + 1], in1=acc, op0=ALU.mult, op1=ALU.add),
                               [Bring[ri], Bwg, Bacc], [Bacc])
                        if nxt is not None and slot % 3 == 2:
                            next(nxt, None)
                    op('pool', L('tensor_tensor', out=tw, in0=acc, in1=g2bc, op=ALU.mult), [Bacc, Bbc2], [Bt])
                    if DEBUG and tg == 0:
                        store(dbg_h2, h2, Bh2); store(dbg_ff, acc, Bacc); store(dbg_e, eidx, Beidx); store(dbg_g, gw.rearrange("p h k -> p (h k)"), Bg)
                else:
                    op('pool', L('memset', tw, 0.0), [Bt], [Bt])
                if nxt is not None:
                    for _ in nxt:
                        pass
                op('dve', L('memset', sm1[:, 0:1], 0.0), [Bsm], [Bsm])
                op('dve', L('scalar_tensor_tensor', out=rw, in0=x1t[s], scalar=float(ALPHA), in1=tw, op0=ALU.mult, op1=ALU.add,
                            accum_out=sm1[:, 0:1]), [Bx1t[s], Bt, Bsm], [Br, Bsm])
                ln_tail(sm1, Bsm, tw, Bt, rw, Br, yw[s], Byw[s], l2g, l2b, Bl2)
                ydst = y_p[tg * 128:(tg + 1) * 128, :] if kind == 'p' else y_s
                store(ydst, yw[s], Byw[s])
            A.release()

        T.finish()
        with nc.Block() as block:
            T.emit(block)
    print("ops:", T.nops, flush=True)
    return nc


def _rope_tables():
    half = 32
    inv = (1.0 / (np.float32(10000.0) ** (np.arange(half, dtype=np.float32) / np.float32(half)))).astype(np.float32)
    posp = np.arange(SEQ, dtype=np.float32)
    angp = (posp[:, None] * inv[None, :]).astype(np.float32)
    poss = (PAST + (np.arange(128) % DSEQ)).astype(np.float32)
    angs = (poss[:, None] * inv[None, :]).astype(np.float32)
    return (np.cos(angp).astype(np.float32), np.sin(angp).astype(np.float32),
            np.cos(angs).astype(np.float32), np.sin(angs).astype(np.float32))


_NC_CACHE = {}


def kernel(x_prompt, x_sample, cache_k, cache_v, cache_conv, c_prompt, c_sample,
           w_ada, b_ada, w_in, b_in, w_dw, b_dw, conv_ln_g, conv_ln_b,
           lam_q1, lam_k1, lam_q2, lam_k2, subln_g, w_out, b_out, ln1_g, ln1_b,
           w_query, sub_keys, u_tab, v_tab, ln2_g, ln2_b):
    f = lambda a: np.ascontiguousarray(np.asarray(a, dtype=np.float32))
    x_prompt, x_sample, cache_k, cache_v, cache_conv = f(x_prompt), f(x_sample), f(cache_k), f(cache_v), f(cache_conv)
    c_prompt, c_sample = f(c_prompt), f(c_sample)
    cosp, sinp, coss, sins = _rope_tables()
    shared = {
        "w_ada": f(w_ada)[0], "b_ada": f(b_ada), "w_in": f(w_in)[0], "b_in": f(b_in), "w_dw": f(w_dw)[0], "b_dw": f(b_dw),
        "cln_g": f(conv_ln_g), "cln_b": f(conv_ln_b),
        "lamv": np.ascontiguousarray(np.concatenate([f(lam_q1), f(lam_k1), f(lam_q2), f(lam_k2)], axis=0)),
        "subln_g": f(subln_g), "w_out": f(w_out)[0], "b_out": f(b_out), "ln1_g": f(ln1_g), "ln1_b": f(ln1_b),
        "w_query": f(w_query)[0], "sub_keys": f(sub_keys).reshape(16 * 128, 128), "u_tab": f(u_tab)[0], "v_tab": f(v_tab)[0],
        "ln2_g": f(ln2_g), "ln2_b": f(ln2_b),
        "ident": np.eye(128, dtype=np.float32), "iota16": np.tile(np.arange(16, dtype=np.float32)[None, :], (128, 1)),
        "cosp": cosp, "sinp": sinp, "coss": coss, "sins": sins,
    }
    in_maps = []
    for i in range(NCORES):
        m = dict(shared)
        m["xp"] = x_prompt[NPS * i:NPS * (i + 1)].reshape(NPS * SEQ, D)
        m["xs"] = x_sample[NSS * i:NSS * (i + 1)].reshape(128, D)
        m["ck"] = cache_k[0, NSS * i:NSS * (i + 1)].reshape(NSS * PAST, 512)
        m["cv"] = cache_v[0, NSS * i:NSS * (i + 1)].reshape(NSS * PAST, 512)
        m["cc"] = cache_conv[0, NSS * i:NSS * (i + 1)].reshape(NSS * HIST, 512)
        m["c6"] = np.ascontiguousarray(np.concatenate([c_prompt[NPS * i:NPS * (i + 1)], c_sample[NSS * i:NSS * (i + 1)]], axis=0))
        in_maps.append(m)
    if "nc" not in _NC_CACHE:
        _NC_CACHE["nc"] = build_program()
    nc = _NC_CACHE["nc"]
    res = run_bass_kernel_spmd(nc, in_maps, core_ids=list(range(NCORES)))
    R = res.results
    if DEBUG:
        _NC_CACHE["dbg"] = {k: np.asarray(R[0][k]) for k in ("dbg_h2", "dbg_ff", "dbg_e", "dbg_g")}
    cat = lambda name: np.concatenate([np.asarray(R[i][name]) for i in range(NCORES)], axis=0)
    y_prompt = cat("y_p").reshape(16, SEQ, D)
    y_sample = cat("y_s").reshape(32, DSEQ, D)
    k_prompt = cat("k_p").reshape(1, 16, SEQ, 4, 2, 64)
    v_prompt = cat("v_p").reshape(1, 16, SEQ, 4, 128)
    conv_prompt = cat("conv_p").reshape(1, 16, HIST, 512)
    k_sample = cat("k_s").reshape(1, 32, DSEQ, 4, 2, 64)
    v_sample = cat("v_s").reshape(1, 32, DSEQ, 4, 128)
    conv_sample = cat("conv_s").reshape(1, 32, HIST, 512)
    return (y_prompt, y_sample, k_prompt, v_prompt, conv_prompt, k_sample, v_sample, conv_sample)
```

```python
import math
from contextlib import ExitStack
import numpy as np
import concourse.bass as bass
import concourse.mybir as mybir
from concourse.bass_utils import run_bass_kernel_spmd

F32 = mybir.dt.float32
BF16 = mybir.dt.bfloat16
I32 = mybir.dt.int32
U32 = mybir.dt.uint32
ALU = mybir.AluOpType
AF = mybir.ActivationFunctionType
AX = mybir.AxisListType

NCORES = 8
D = 1024
SEQ = 2048
NPS = 2
NSS = 4
DSEQ = 32
PAST = 2048
CW = 31
HIST = CW - 1
ALPHA = 2.0 ** 0.25
LAM_INIT = 0.8 - 0.6 * math.exp(0.0)
EPS = 1e-5
NEXP = 16384
STAGE = 9
PEER_ON = True
CONV_PE = True
NRING = 8
DEBUG = False

ENGS = ['pe', 'act', 'dve', 'pool', 'sp']
SAME_SYNC = {'pe': False, 'act': True, 'dve': True, 'pool': True, 'sp': False}


class Buf:
    __slots__ = ('name', 'w', 'r')

    def __init__(self, name=''):
        self.name = name
        self.w = None
        self.r = {}


class Tracker:
    def __init__(self, nc, es, n_dma_sems=32):
        self.nc = nc
        self.streams = {e: [] for e in ENGS}
        self.sems = {}
        for e in ['pe', 'act', 'dve', 'pool']:
            self.sems[e] = es.enter_context(nc.semaphore('s_' + e))
        for i in range(n_dma_sems):
            self.sems[('d', i)] = es.enter_context(nc.semaphore('d%d' % i))
        self.n_dma = n_dma_sems
        self.cnt = {k: 0 for k in self.sems}
        self.known = {e: {} for e in ENGS}
        self.rr = {'sp': 0, 'pool': 0, 'act': 0}
        self.nops = {e: 0 for e in ENGS}

    def _need(self, eng, key, val, force=False):
        if val <= 0:
            return
        if key == eng and not (SAME_SYNC[eng] or force):
            return
        if self.known[eng].get(key, 0) >= val:
            return
        self.known[eng][key] = val
        sem = self.sems[key]
        self.streams[eng].append(L('wait_ge', sem, val))

    def _deps(self, eng, reads, writes):
        for b in reads:
            if b.w is not None:
                self._need(eng, *b.w)
        for b in writes:
            if b.w is not None:
                self._need(eng, *b.w)
            for k, v in b.r.items():
                self._need(eng, k, v)

    def _mark(self, key, v, reads, writes):
        for b in reads:
            if b.r.get(key, 0) < v:
                b.r[key] = v
        for b in writes:
            b.w = (key, v)
            b.r = {}

    def op(self, eng, fn, reads=(), writes=()):
        self._deps(eng, reads, writes)
        self.cnt[eng] += 1
        v = self.cnt[eng]
        sem = self.sems[eng]
        self.streams[eng].append(lambda e, fn=fn, sem=sem: fn(e).then_inc(sem, 1))
        self.nops[eng] += 1
        self._mark(eng, v, reads, writes)

    def dma(self, q, fn, reads=(), writes=()):
        half = self.n_dma // 2
        i = self.rr[q]
        self.rr[q] = (i + 1) % half
        key = ('d', i + (half if q == 'pool' else 0))
        prev = self.cnt[key]
        self._need(q, key, prev)
        self._deps(q, reads, writes)
        self.cnt[key] = prev + 16
        sem = self.sems[key]
        self.streams[q].append(lambda e, fn=fn, sem=sem: fn(e).then_inc(sem, 16))
        self.nops[q] += 1
        self._mark(key, prev + 16, reads, writes)

    def barrier(self):
        for e in ENGS:
            for key, v in self.cnt.items():
                if key != e:
                    self._need(e, key, v)

    def finish(self):
        for key, v in self.cnt.items():
            self._need('sp', key, v)

    def emit(self, block):
        st = self.streams

        @block.sync
        def _(e):
            for f in st['sp']:
                f(e)

        @block.tensor
        def _(e):
            for f in st['pe']:
                f(e)

        @block.scalar
        def _(e):
            for f in st['act']:
                f(e)

        @block.vector
        def _(e):
            for f in st['dve']:
                f(e)

        @block.gpsimd
        def _(e):
            for f in st['pool']:
                f(e)


def L(name, *args, **kw):
    return lambda e: getattr(e, name)(*args, **kw)


class Arena:
    def __init__(self, ap_all, nwords):
        self.ap = ap_all
        self.n = nwords
        self.top = 0
        self.marks = []

    def alloc(self, shape, dt=F32):
        per = 1
        for s in shape[1:]:
            per *= s
        nb = per * (4 if dt in (F32, I32, U32) else 2)
        nw = (nb + 3) // 4
        nw = (nw + 7) // 8 * 8
        off = self.top
        self.top += nw
        assert self.top <= self.n, "arena overflow %d > %d" % (self.top, self.n)
        v = self.ap[0:shape[0], off:off + nw]
        if dt != F32:
            v = v.bitcast(dt)
        v = v[:, 0:per]
        if len(shape) == 3:
            v = v.rearrange("p (a b) -> p a b", a=shape[1])
        elif len(shape) == 4:
            v = v.rearrange("p (a b c) -> p a b c", a=shape[1], b=shape[2])
        return v

    def mark(self):
        self.marks.append(self.top)

    def release(self):
        self.top = self.marks.pop()


def build_program():
    nc = bass.Bass("TRN2", target_bir_lowering=False)

    def din(name, shape, dt=F32):
        return nc.dram_tensor(name, list(shape), dt, kind="ExternalInput").ap()

    def dout(name, shape, dt=F32):
        return nc.dram_tensor(name, list(shape), dt, kind="ExternalOutput").ap()

    xp = din("xp", [NPS * SEQ, D])
    xs = din("xs", [128, D])
    ck = din("ck", [NSS * PAST, 512])
    cv = din("cv", [NSS * PAST, 512])
    cc = din("cc", [NSS * HIST, 512])
    c6 = din("c6", [6, D])
    w_ada = din("w_ada", [D, 6 * D])
    b_ada = din("b_ada", [1, 6 * D])
    w_in = din("w_in", [D, 2560])
    b_in = din("b_in", [1, 2560])
    w_dw = din("w_dw", [CW, 512])
    b_dw = din("b_dw", [1, 512])
    cln_g = din("cln_g", [1, 512])
    cln_b = din("cln_b", [1, 512])
    lamv = din("lamv", [4, 64])
    subln_g = din("subln_g", [1, 128])
    w_out = din("w_out", [D, D])
    b_out = din("b_out", [1, D])
    ln1_g = din("ln1_g", [1, D])
    ln1_b = din("ln1_b", [1, D])
    w_query = din("w_query", [D, 2048])
    sub_keys = din("sub_keys", [16 * 128, 128])
    u_tab = din("u_tab", [NEXP, D])
    v_tab = din("v_tab", [NEXP, D])
    ln2_g = din("ln2_g", [1, D])
    ln2_b = din("ln2_b", [1, D])
    ident_d = din("ident", [128, 128])
    iota_d = din("iota16", [128, 16])
    cosp_d = din("cosp", [SEQ, 32])
    sinp_d = din("sinp", [SEQ, 32])
    coss_d = din("coss", [128, 32])
    sins_d = din("sins", [128, 32])

    y_p = dout("y_p", [NPS * SEQ, D])
    y_s = dout("y_s", [128, D])
    k_p = dout("k_p", [NPS * SEQ, 512])
    v_p = dout("v_p", [NPS * SEQ, 512])
    conv_p = dout("conv_p", [NPS * HIST, 512])
    k_s = dout("k_s", [128, 512])
    v_s = dout("v_s", [128, 512])
    conv_s = dout("conv_s", [NSS * HIST, 512])

    if DEBUG:
        dbg_h2 = dout("dbg_h2", [128, D]); dbg_ff = dout("dbg_ff", [128, D]); dbg_e = dout("dbg_e", [128, 128], I32); dbg_g = dout("dbg_g", [128, 128])
    modscr = nc.dram_tensor("modscr", [6, 4096], F32, kind="Internal").ap()
    x1scr = nc.dram_tensor("x1scr", [NPS * SEQ + 128, D], F32, kind="Internal").ap()

    with ExitStack() as es:
        T = Tracker(nc, es, n_dma_sems=32)
        AW = 52600
        arena_t = es.enter_context(nc.sbuf_tensor("arena", [128, AW], F32))
        A = Arena(arena_t[:, :], AW)
        ps = es.enter_context(nc.psum_tensor("ps", [128, 8, 512], F32))
        PB = [Buf("ps%d" % i) for i in range(8)]

        def op(eng, fn, r=(), w=()):
            T.op(eng, fn, r, w)

        def dma(fn, r=(), w=(), q='sp'):
            T.dma(q, fn, r, w)

        def load(out_ap, in_ap, wb, q='sp'):
            T.dma(q, L('dma_start', out=out_ap, in_=in_ap), (), [wb])

        def store(out_ap, in_ap, rb, wbufs=(), q='sp'):
            T.dma(q, L('dma_start', out=out_ap, in_=in_ap), [rb], list(wbufs))

        def mm(out_ap, lhsT, rhs, start, stop, r, w):
            T.op('pe', L('matmul', out_ap, lhsT=lhsT, rhs=rhs, start=start, stop=stop), r, w)

        def tr(out_ap, in_ap, k, r, w):
            T.op('pe', L('transpose', out_ap, in_ap, ident[0:k, 0:k]), list(r) + [Bc], w)

        def ln_tail(sm1, Bsm, tw, Bt, rw_, Br_, out_ap, Bout, gbc, bbc, Bg):
            op('dve', L('memset', sm1[:, 1:2], 0.0), [Bsm], [Bsm])
            op('dve', L('scalar_tensor_tensor', out=tw, in0=rw_, scalar=1.0, in1=rw_, op0=ALU.mult, op1=ALU.mult, accum_out=sm1[:, 1:2]),
               [Br_, Bsm], [Bt, Bsm])
            op('dve', L('tensor_scalar', out=sm1[:, 2:3], in0=sm1[:, 0:1], scalar1=-1.0 / D, scalar2=None, op0=ALU.mult), [Bsm], [Bsm])
            op('dve', L('tensor_tensor', out=sm1[:, 3:4], in0=sm1[:, 2:3], in1=sm1[:, 2:3], op=ALU.mult), [Bsm], [Bsm])
            op('dve', L('scalar_tensor_tensor', out=sm1[:, 4:5], in0=sm1[:, 1:2], scalar=1.0 / D, in1=sm1[:, 3:4], op0=ALU.mult, op1=ALU.subtract),
               [Bsm], [Bsm])
            op('act', L('activation', out=sm1[:, 5:6], in_=sm1[:, 4:5], func=AF.Sqrt, bias=EPS, scale=1.0), [Bsm], [Bsm])
            op('dve', L('reciprocal', out=sm1[:, 6:7], in_=sm1[:, 5:6]), [Bsm], [Bsm])
            op('dve', L('tensor_scalar', out=tw, in0=rw_, scalar1=sm1[:, 2:3], scalar2=sm1[:, 6:7], op0=ALU.add, op1=ALU.mult), [Br_, Bsm, Bt], [Bt])
            op('pool', L('tensor_tensor', out=tw, in0=tw, in1=gbc, op=ALU.mult), [Bt, Bg], [Bt])
            op('pool', L('tensor_tensor', out=out_ap, in0=tw, in1=bbc, op=ALU.add), [Bt, Bg], [Bout])

        ident = A.alloc([128, 128])
        onesdiv = A.alloc([128, 128])
        iota16 = A.alloc([128, 16])
        smallT = A.alloc([128, 40])
        wdwT = A.alloc([128, 4, CW])
        modT = A.alloc([128, 16, 6])
        lam_t = A.alloc([128, 4])
        gsub = A.alloc([128, 128])
        Bc = Buf("consts")
        Bmod = Buf("modT")
        load(ident, ident_d, Bc)
        load(iota16, iota_d, Bc)
        op('dve', L('memset', onesdiv, 1.0 / 512.0), (), [Bc])

        A.mark()
        stg = A.alloc([128, 128])
        Bstg = Buf("stg")
        op('pool', L('memset', stg, 0.0), (), [Bstg])
        load(stg[0:16, :], b_ada[0:1, 0:2048].rearrange("o (c p) -> (o c) p", p=128), Bstg)
        load(stg[16:24, :], b_in[0:1, 0:1024].rearrange("o (c p) -> (o c) p", p=128), Bstg)
        load(stg[24:28, :], b_dw[0:1, :].rearrange("o (c p) -> (o c) p", p=128), Bstg)
        load(stg[28:32, :], cln_g[0:1, :].rearrange("o (c p) -> (o c) p", p=128), Bstg)
        load(stg[32:36, :], cln_b[0:1, :].rearrange("o (c p) -> (o c) p", p=128), Bstg)
        tr(ps[:, 0, 0:40], stg[0:40, :], 40, [Bstg], [PB[0]])
        op('dve', L('tensor_copy', out=smallT, in_=ps[:, 0, 0:40]), [PB[0]], [Bc])
        wdw_s = A.alloc([CW, 512])
        Bw = Buf("wdw_s")
        load(wdw_s, w_dw, Bw)
        for c in range(4):
            tr(ps[:, 1, c * 32:c * 32 + CW], wdw_s[0:CW, c * 128:(c + 1) * 128], CW, [Bw], [PB[1]])
        op('dve', L('tensor_copy', out=wdwT, in_=ps[:, 1, 0:128].rearrange("p (c j) -> p c j", c=4)[:, :, 0:CW]), [PB[1]], [Bc])
        lam_s = A.alloc([128, 4, 64])
        Bl = Buf("lam_s")
        for i in range(4):
            load(lam_s[:, i, :], lamv[i:i + 1, :].partition_broadcast(128), Bl)
        lam_j = A.alloc([128, 64])
        lam_a = A.alloc([128, 4])
        Blj = Buf("lamj")
        op('dve', L('memset', lam_a, 0.0), (), [Blj])
        for i in range(2):
            op('dve', L('scalar_tensor_tensor', out=lam_j, in0=lam_s[:, 2 * i, :], scalar=1.0, in1=lam_s[:, 2 * i + 1, :],
                                                             op0=ALU.mult, op1=ALU.mult, accum_out=lam_a[:, i:i + 1]), [Bl, Blj], [Blj])
        op('act', L('activation', out=lam_a[:, 2:4], in_=lam_a[:, 0:2], func=AF.Exp), [Blj], [Blj])
        op('dve', L('tensor_tensor', out=lam_t[:, 0:1], in0=lam_a[:, 2:3], in1=lam_a[:, 3:4], op=ALU.subtract), [Blj], [Bc])
        op('dve', L('tensor_scalar', out=lam_t[:, 0:1], in0=lam_t[:, 0:1], scalar1=float(LAM_INIT), scalar2=None, op0=ALU.add), [Bc], [Bc])
        op('dve', L('tensor_scalar', out=lam_t[:, 1:2], in0=lam_t[:, 0:1], scalar1=-1.0, scalar2=None, op0=ALU.mult), [Bc], [Bc])
        load(gsub, subln_g[0:1, :].partition_broadcast(128), Bc)
        op('dve', L('tensor_scalar', out=gsub, in0=gsub, scalar1=float(1.0 - LAM_INIT), scalar2=None, op0=ALU.mult), [Bc], [Bc])

        c6s = A.alloc([6, D])
        Bc6 = Buf("c6")
        load(c6s, c6, Bc6)
        op('act', L('activation', out=c6s, in_=c6s, func=AF.Silu), [Bc6], [Bc6])
        for c in range(8):
            tr(ps[:, 2, c * 6:(c + 1) * 6], c6s[0:6, c * 128:(c + 1) * 128], 6, [Bc6], [PB[2]])
        scT = A.alloc([128, 8, 6], BF16)
        BscT = Buf("scT")
        op('dve', L('tensor_copy', out=scT, in_=ps[:, 2, 0:48].rearrange("p (c s) -> p c s", c=8)), [PB[2]], [BscT])
        bada6 = A.alloc([6, 4096])
        Bb6 = Buf("bada6")
        load(bada6, b_ada[0:1, 2048:6144].partition_broadcast(6), Bb6)
        modtok = A.alloc([6, 4096])
        Bmt = Buf("modtok")
        wst = [A.alloc([128, 8, 512]) for _ in range(2)]
        wbf = [A.alloc([128, 8, 512], BF16) for _ in range(2)]
        Bwst = [Buf("wst0"), Buf("wst1")]
        Bwbf = [Buf("wbf0"), Buf("wbf1")]
        w_ada_v = w_ada.rearrange("(k p) n -> p k n", p=128)
        cast_eng = ['dve', 'pool']
        for cb in range(12):
            s = cb % 2
            load(wst[s], w_ada_v[:, :, cb * 512:(cb + 1) * 512], Bwst[s])
            if cast_eng[s] == 'dve':
                op('dve', L('tensor_copy', out=wbf[s], in_=wst[s]), [Bwst[s]], [Bwbf[s]])
            else:
                op('pool', L('tensor_copy', out=wbf[s], in_=wst[s]), [Bwst[s]], [Bwbf[s]])
            if cb < 4:
                for cc_ in range(4):
                    ch = cb * 4 + cc_
                    for k in range(8):
                        mm(ps[:, 3, ch * 6:(ch + 1) * 6], wbf[s][:, k, cc_ * 128:(cc_ + 1) * 128], scT[:, k, :], k == 0, k == 7,
                           [Bwbf[s], BscT], [PB[3]])
            else:
                pb = 4 + cb % 2
                for k in range(8):
                    mm(ps[0:6, pb, :], scT[:, k, :], wbf[s][:, k, :], k == 0, k == 7, [Bwbf[s], BscT], [PB[pb]])
                o0 = (cb - 4) * 512
                op('dve', L('tensor_tensor', out=modtok[:, o0:o0 + 512], in0=ps[0:6, pb, :], in1=bada6[:, o0:o0 + 512],
                                                                  op=ALU.add), [PB[pb], Bb6], [Bmt])
        for ch in range(16):
            if ch < 8:
                op('dve', L('tensor_scalar', out=modT[:, ch, :], in0=ps[:, 3, ch * 6:(ch + 1) * 6], scalar1=smallT[:, ch:ch + 1],
                                                           scalar2=None, op0=ALU.add), [PB[3], Bc], [Bmod])
            else:
                op('dve', L('tensor_scalar', out=modT[:, ch, :], in0=ps[:, 3, ch * 6:(ch + 1) * 6], scalar1=smallT[:, ch:ch + 1],
                                                           scalar2=1.0, op0=ALU.add, op1=ALU.add), [PB[3], Bc], [Bmod])
        op('dve', L('tensor_scalar', out=modtok[:, 2048:3072], in0=modtok[:, 2048:3072], scalar1=1.0, scalar2=None, op0=ALU.add), [Bmt], [Bmt])
        Bscr = Buf("modscr")
        store(modscr, modtok, Bmt, [Bscr])
        T.barrier()
        A.release()

        top_consts = A.top
        qT = A.alloc([128, 4, SEQ], BF16)
        kT = A.alloc([128, 4, SEQ], BF16)
        Vp = A.alloc([128, 16, 4, 130], BF16)
        u_ext = A.alloc([128, 4, HIST + SEQ])
        kTn = A.alloc([128, 4, 128], BF16)
        Vn = A.alloc([32, 4, 4, 130], BF16)
        BqT, BkT, BVp, Bu, Bmix, BkTn, BVn = [Buf(n) for n in ["qT", "kT", "Vp", "u", "mix", "kTn", "Vn"]]
        op('pool', L('memset', Vp[:, :, :, 128:130], 1.0), (), [BVp])
        op('pool', L('memset', Vn[:, :, :, 128:130], 1.0), (), [BVn])
        Bx1 = [Buf("x1s%d" % i) for i in range(33)]

        units = [dict(kind='p', idx=0), dict(kind='p', idx=1), dict(kind='s', idx=0)]
        if STAGE < 9:
            units = units[:1] + units[2:]

        for U in units:
            isP = U['kind'] == 'p'
            ntile = 16 if isP else 1
            ntok = ntile * 128
            TB = 512 if isP else 128
            nblk = ntok // TB
            if isP:
                x_src = xp[U['idx'] * SEQ:(U['idx'] + 1) * SEQ, :]
                y_dst = y_p[U['idx'] * SEQ:(U['idx'] + 1) * SEQ, :]
                k_dst = k_p[U['idx'] * SEQ:(U['idx'] + 1) * SEQ, :]
                v_dst = v_p[U['idx'] * SEQ:(U['idx'] + 1) * SEQ, :]
                segs = [(U['idx'], 0, SEQ)]
                tile0 = U['idx'] * 16
            else:
                x_src, y_dst, k_dst, v_dst = xs, y_s, k_s, v_s
                segs = [(2 + j, 32 * j, 32) for j in range(4)]
                tile0 = 32
            UW = (HIST + SEQ) if isP else 4 * 62

            def ucols(c, j0, n, b):
                if isP:
                    return u_ext[:, c, b * 512 + j0:b * 512 + j0 + 512]
                return u_ext[:, c, 0:248].rearrange("p (s t) -> p s t", s=4)[:, :, j0:j0 + 32]

            A.mark()
            A.mark()
            w_in_bf = A.alloc([128, 8, 2560], BF16)
            Bwin = Buf("w_in_bf")
            wst = [A.alloc([128, 8, 256]) for _ in range(2)]
            Bwst = [Buf("wst0"), Buf("wst1")]
            w_in_v = w_in.rearrange("(k p) n -> p k n", p=128)
            for cb in range(10):
                s = cb % 2
                load(wst[s], w_in_v[:, :, cb * 256:(cb + 1) * 256], Bwst[s])
                eng = 'dve' if s == 0 else 'pool'
                op(eng, L('tensor_copy', out=w_in_bf[:, :, cb * 256:(cb + 1) * 256], in_=wst[s]), [Bwst[s]], [Bwin])
            binbc = A.alloc([128, 1536])
            Bbin = Buf("binbc")
            load(binbc, b_in[0:1, 1024:2560].partition_broadcast(128), Bbin)
            cos_t = A.alloc([128, ntile, 32])
            sin_t = A.alloc([128, ntile, 32])
            Brope = Buf("rope")
            if isP:
                load(cos_t, cosp_d.rearrange("(t p) f -> p t f", p=128), Brope)
                load(sin_t, sinp_d.rearrange("(t p) f -> p t f", p=128), Brope)
            else:
                load(cos_t[:, 0, :], coss_d, Brope)
                load(sin_t[:, 0, :], sins_d, Brope)
            if isP:
                op('pool', L('memset', u_ext[:, :, 0:HIST], 0.0), (), [Bu])
            else:
                hst = A.alloc([HIST, 512])
                Bh = Buf("hst")
                for j in range(4):
                    load(hst, cc[j * HIST:(j + 1) * HIST, :], Bh)
                    for c in range(4):
                        tr(ps[:, 0, c * 32:c * 32 + HIST], hst[0:HIST, c * 128:(c + 1) * 128], HIST, [Bh], [PB[0]])
                    op('dve', L('tensor_copy', out=u_ext[:, :, 62 * j:62 * j + HIST],
                                                           in_=ps[:, 0, 0:128].rearrange("p (c t) -> p c t", c=4)[:, :, 0:HIST]), [PB[0]], [Bu])
            xst = [A.alloc([128, D]) for _ in range(2)]
            Bxst = [Buf("xst0"), Buf("xst1")]
            hTs = [A.alloc([128, 8, TB], BF16) for _ in range(2)]
            BhTs = [Buf("hT0"), Buf("hT1")]
            sig_t = A.alloc([128, 512])
            Bsig = Buf("sig")
            qb = A.alloc([128, 512]); kb = A.alloc([128, 512]); vb = A.alloc([128, 512])
            qr = A.alloc([128, 512]); kr = A.alloc([128, 512])
            rt1 = A.alloc([128, 256]); rt2 = A.alloc([128, 256])
            Bqb, Bkb, Bvb, Bqr, Bkr, Brt = [Buf(n) for n in ["qb", "kb", "vb", "qr", "kr", "rt"]]

            def rope(src, dst, Bs, Bd, t):
                s4 = src.rearrange("p (g two f) -> p g two f", g=8, two=2)
                d4 = dst.rearrange("p (g two f) -> p g two f", g=8, two=2)
                X1, X2 = s4[:, :, 0, :], s4[:, :, 1, :]
                O1, O2 = d4[:, :, 0, :], d4[:, :, 1, :]
                Cb = cos_t[:, t, :].unsqueeze(1).broadcast_to([128, 8, 32])
                Sb = sin_t[:, t, :].unsqueeze(1).broadcast_to([128, 8, 32])
                a1 = rt1.rearrange("p (g f) -> p g f", g=8)
                a2 = rt2.rearrange("p (g f) -> p g f", g=8)
                op('pool', L('tensor_tensor', out=a1, in0=X1, in1=Cb, op=ALU.mult), [Bs, Brope], [Brt])
                op('pool', L('tensor_tensor', out=a2, in0=X2, in1=Sb, op=ALU.mult), [Bs, Brope, Brt], [Brt])
                op('pool', L('tensor_tensor', out=O1, in0=a1, in1=a2, op=ALU.subtract), [Brt], [Bd])
                op('pool', L('tensor_tensor', out=a1, in0=X2, in1=Cb, op=ALU.mult), [Bs, Brope, Bd], [Brt])
                op('pool', L('tensor_tensor', out=a2, in0=X1, in1=Sb, op=ALU.mult), [Bs, Brope, Brt], [Brt])
                op('pool', L('tensor_tensor', out=O2, in0=a1, in1=a2, op=ALU.add), [Brt, Bd], [Bd])

            for b in range(nblk):
                nt = TB // 128
                hT, BhT = hTs[b % 2], BhTs[b % 2]
                for tt in range(nt):
                    t = b * nt + tt
                    s = t % 2
                    load(xst[s], x_src[t * 128:(t + 1) * 128, :], Bxst[s])
                    for c in range(8):
                        tr(ps[:, c // 4, (c % 4) * 128:(c % 4 + 1) * 128], xst[s][:, c * 128:(c + 1) * 128], 128, [Bxst[s]], [PB[c // 4]])
                    for c in range(8):
                        for (sq_, t0, n) in segs:
                            lo = max(t0, t * 128)
                            hi = min(t0 + n, (t + 1) * 128)
                            if lo >= hi:
                                continue
                            l0 = lo - t * 128
                            w_ = hi - lo
                            op('dve', L('tensor_scalar',
                                out=hT[:, c, tt * 128 + l0:tt * 128 + l0 + w_],
                                in0=ps[:, c // 4, (c % 4) * 128 + l0:(c % 4) * 128 + l0 + w_],
                                scalar1=modT[:, 8 + c, sq_:sq_ + 1], scalar2=modT[:, c, sq_:sq_ + 1], op0=ALU.mult, op1=ALU.add),
                               [PB[c // 4], Bmod], [BhT])
                for oc in range(4):
                    for k in range(8):
                        mm(ps[:, 2, 0:TB], w_in_bf[:, k, (4 + oc) * 128:(5 + oc) * 128], hT[:, k, :], k == 0, k == 7, [Bwin, BhT], [PB[2]])
                    for k in range(8):
                        mm(ps[:, 3, 0:TB], w_in_bf[:, k, oc * 128:(oc + 1) * 128], hT[:, k, :], k == 0, k == 7, [Bwin, BhT], [PB[3]])
                    op('act', L('activation', out=sig_t[:, 0:TB], in_=ps[:, 2, 0:TB], func=AF.Sigmoid,
                                                            bias=smallT[:, 20 + oc:21 + oc]), [PB[2], Bc], [Bsig])
                    if isP:
                        o_ap = u_ext[:, oc, HIST + b * 512:HIST + (b + 1) * 512]
                        i0 = ps[:, 3, 0:512]
                        i1 = sig_t[:, 0:512]
                    else:
                        o_ap = u_ext[:, oc, 0:248].rearrange("p (s t) -> p s t", s=4)[:, :, HIST:62]
                        i0 = ps[:, 3, 0:128].rearrange("p (s t) -> p s t", s=4)
                        i1 = sig_t[:, 0:128].rearrange("p (s t) -> p s t", s=4)
                    op('dve', L('scalar_tensor_tensor',
                        out=o_ap, in0=i0, scalar=smallT[:, 16 + oc:17 + oc], in1=i1, op0=ALU.add, op1=ALU.mult), [PB[3], Bsig, Bc], [Bu])
                for tt in range(nt):
                    t = b * nt + tt
                    hs = hT[:, :, tt * 128:(tt + 1) * 128]
                    for part, pbk in ((0, 4), (1, 5), (2, 6)):
                        for k in range(8):
                            mm(ps[:, pbk, :], hs[:, k, :], w_in_bf[:, k, 1024 + part * 512:1536 + part * 512], k == 0, k == 7, [Bwin, BhT], [PB[pbk]])
                    op('dve', L('tensor_tensor', out=qb, in0=ps[:, 4, :], in1=binbc[:, 0:512], op=ALU.add), [PB[4], Bbin], [Bqb])
                    op('dve', L('tensor_tensor', out=kb, in0=ps[:, 5, :], in1=binbc[:, 512:1024], op=ALU.add), [PB[5], Bbin], [Bkb])
                    op('dve', L('tensor_tensor', out=vb, in0=ps[:, 6, :], in1=binbc[:, 1024:1536], op=ALU.add), [PB[6], Bbin], [Bvb])
                    rope(qb, qr, Bqb, Bqr, t)
                    rope(kb, kr, Bkb, Bkr, t)
                    store(k_dst[t * 128:(t + 1) * 128, :], kr, Bkr, q='act')
                    store(v_dst[t * 128:(t + 1) * 128, :], vb, Bvb, q='act')
                    for h in range(4):
                        tr(ps[:, 7, h * 128:(h + 1) * 128], qr[:, h * 128:(h + 1) * 128], 128, [Bqr], [PB[7]])
                    op('act', L('copy', out=qT[:, :, t * 128:(t + 1) * 128], in_=ps[:, 7, :].rearrange("p (h n) -> p h n", h=4)), [PB[7]], [BqT])
                    for h in range(4):
                        tr(ps[:, 7, h * 128:(h + 1) * 128], kr[:, h * 128:(h + 1) * 128], 128, [Bkr], [PB[7]])
                    if isP:
                        op('act', L('copy', out=kT[:, :, t * 128:(t + 1) * 128], in_=ps[:, 7, :].rearrange("p (h n) -> p h n", h=4)), [PB[7]], [BkT])
                        op('act', L('copy', out=Vp[:, t, :, 0:128], in_=vb.rearrange("p (h n) -> p h n", h=4)), [Bvb], [BVp])
                    else:
                        op('act', L('copy', out=kTn, in_=ps[:, 7, :].rearrange("p (h n) -> p h n", h=4)), [PB[7]], [BkTn])
                        for j in range(4):
                            for k in range(8):
                                mm(ps[0:32, 6, :], hT[:, k, 32 * j:32 * j + 32], w_in_bf[:, k, 2048:2560], k == 0, k == 7, [Bwin, BhT], [PB[6]])
                            op('dve', L('tensor_tensor', out=Vn[:, j, :, 0:128], in0=ps[0:32, 6, :].rearrange("p (h n) -> p h n", h=4),
                                                                     in1=binbc[0:32, 1024:1536].rearrange("p (h n) -> p h n", h=4), op=ALU.add),
                               [PB[6], Bbin], [BVn])
            cst = A.alloc([HIST, 512])
            Bcst = Buf("cst")
            for (sq_, t0, n) in segs:
                if isP:
                    c0 = SEQ
                    dst = conv_p[U['idx'] * HIST:(U['idx'] + 1) * HIST, :]
                else:
                    j = sq_ - 2
                    c0 = 62 * j + 32
                    dst = conv_s[j * HIST:(j + 1) * HIST, :]
                for c in range(4):
                    tr(ps[0:HIST, 0, c * 128:(c + 1) * 128], u_ext[:, c, c0:c0 + HIST], 128, [Bu], [PB[0]])
                op('dve', L('tensor_copy', out=cst, in_=ps[0:HIST, 0, :]), [PB[0]], [Bcst])
                store(dst, cst, Bcst)
            T.barrier()
            A.release()
            if STAGE < 2:
                A.release()
                continue
            mixT = A.alloc([128, 8, SEQ], BF16)

            A.mark()
            NB = TB
            dw = A.alloc([128, 4, NB])
            sq = A.alloc([128, 4, NB])
            mean_sb = A.alloc([128, NB]); m2 = A.alloc([128, NB]); rstd = A.alloc([128, NB])
            Bdw = [Buf("dw%d" % c) for c in range(4)]
            Bsq, Bst = Buf("sq"), Buf("stat")

            def v3(ap):
                return ap if isP else ap.rearrange("p (s t) -> p s t", s=4)
            if isP and CONV_PE:
                u_bf = A.alloc([128, 4, HIST + SEQ], BF16)
                diagw = A.alloc([128, 4, CW, 128], BF16)
                Bubf, Bdiag = Buf("u_bf"), Buf("diagw")
                for c in range(4):
                    op('act', L('copy', out=u_bf[:, c, :], in_=u_ext[:, c, :]), [Bu], [Bubf])
                    op('dve', L('tensor_tensor', out=diagw[:, c, :, :], in0=ident.unsqueeze(1).broadcast_to([128, CW, 128]),
                                in1=wdwT[:, c, :].unsqueeze(2).broadcast_to([128, CW, 128]), op=ALU.mult), [Bc], [Bdiag])
            crot = 0
            for b in range(nblk):
                for c in range(4):
                    if isP and CONV_PE:
                        bank = 2 + crot % 4
                        crot += 1
                        for j in range(CW):
                            mm(ps[:, bank, 0:512], diagw[:, c, j, :], u_bf[:, c, b * 512 + j:b * 512 + j + 512], j == 0, j == CW - 1,
                               [Bdiag, Bubf], [PB[bank]])
                        op('act', L('activation', out=dw[:, c, :], in_=ps[:, bank, 0:512], func=AF.Identity, bias=smallT[:, 24 + c:25 + c]),
                           [PB[bank], Bc], [Bdw[c]])
                        continue
                    eng = 'dve'
                    o_ap = v3(dw[:, c, :])
                    op(eng, L('tensor_scalar', out=o_ap, in0=ucols(c, 0, NB, b), scalar1=wdwT[:, c, 0:1],
                                                                           scalar2=smallT[:, 24 + c:25 + c], op0=ALU.mult, op1=ALU.add),
                       [Bu, Bc], [Bdw[c]])
                    for j in range(1, CW):
                        op(eng, L('scalar_tensor_tensor', out=o_ap, in0=ucols(c, j, NB, b), scalar=wdwT[:, c, j:j + 1],
                                                                                           in1=o_ap, op0=ALU.mult, op1=ALU.add), [Bu, Bc, Bdw[c]], [Bdw[c]])
                op('act', L('activation', out=sq, in_=dw, func=AF.Square), Bdw, [Bsq])
                for c in range(4):
                    mm(ps[:, 0, 0:NB], onesdiv, dw[:, c, :], c == 0, c == 3, [Bc] + Bdw, [PB[0]])
                for c in range(4):
                    mm(ps[:, 1, 0:NB], onesdiv, sq[:, c, :], c == 0, c == 3, [Bc, Bsq], [PB[1]])
                op('dve', L('tensor_copy', out=mean_sb, in_=ps[:, 0, 0:NB]), [PB[0]], [Bst])
                op('dve', L('tensor_tensor', out=m2, in0=mean_sb, in1=mean_sb, op=ALU.mult), [Bst], [Bst])
                op('dve', L('tensor_tensor', out=m2, in0=ps[:, 1, 0:NB], in1=m2, op=ALU.subtract), [PB[1], Bst], [Bst])
                op('act', L('activation', out=rstd, in_=m2, func=AF.Sqrt, bias=EPS, scale=1.0), [Bst], [Bst])
                op('dve', L('reciprocal', out=rstd, in_=rstd), [Bst], [Bst])
                for c in range(4):
                    eng = 'dve' if c % 2 == 0 else 'pool'
                    op(eng, L('tensor_tensor', out=dw[:, c, :], in0=dw[:, c, :], in1=mean_sb, op=ALU.subtract), [Bst, Bdw[c]], [Bdw[c]])
                    op(eng, L('tensor_tensor', out=dw[:, c, :], in0=dw[:, c, :], in1=rstd, op=ALU.mult), [Bst, Bdw[c]], [Bdw[c]])
                    op('act', L('activation', out=mixT[:, c, b * NB:(b + 1) * NB], in_=dw[:, c, :], func=AF.Silu,
                                                               bias=smallT[:, 32 + c:33 + c], scale=smallT[:, 28 + c:29 + c]), [Bdw[c], Bc], [Bmix])
            T.barrier()
            A.release()
            if STAGE < 3:
                A.release()
                continue

            A.mark()
            o1n = A.alloc([128, 128]); oc_ = A.alloc([128, 128]); junk = A.alloc([128, 128])
            sm = A.alloc([128, 8])
            Bo = Buf("o_work")
            att = A.alloc([128, 4, 512])
            Batt = Buf("att")

            def combine(np_, pb1, pb2, att_ap):
                P = slice(0, np_)
                op('dve', L('reciprocal', out=sm[P, 0:1], in_=ps[P, pb1, 128:129]), [PB[pb1]], [Bo])
                op('dve', L('reciprocal', out=sm[P, 1:2], in_=ps[P, pb2, 128:129]), [PB[pb2], Bo], [Bo])
                op('dve', L('tensor_tensor', out=sm[P, 1:2], in0=sm[P, 1:2], in1=lam_t[P, 1:2], op=ALU.mult), [Bo, Bc], [Bo])
                op('dve', L('tensor_scalar', out=o1n[P, :], in0=ps[P, pb1, 0:128], scalar1=sm[P, 0:1], scalar2=None, op0=ALU.mult), [PB[pb1], Bo], [Bo])
                op('dve', L('scalar_tensor_tensor', out=oc_[P, :], in0=ps[P, pb2, 0:128], scalar=sm[P, 1:2], in1=o1n[P, :],
                                                           op0=ALU.mult, op1=ALU.add), [PB[pb2], Bo], [Bo])
                op('dve', L('memset', sm[P, 2:3], 0.0), [Bo], [Bo])
                op('dve', L('scalar_tensor_tensor', out=junk[P, :], in0=oc_[P, :], scalar=1.0, in1=oc_[P, :], op0=ALU.mult, op1=ALU.mult,
                                                           accum_out=sm[P, 2:3]), [Bo], [Bo])
                op('act', L('activation', out=sm[P, 3:4], in_=sm[P, 2:3], func=AF.Sqrt, bias=EPS, scale=1.0 / 128.0), [Bo], [Bo])
                op('dve', L('reciprocal', out=sm[P, 4:5], in_=sm[P, 3:4]), [Bo], [Bo])
                op('dve', L('scalar_tensor_tensor', out=att_ap, in0=oc_[P, :], scalar=sm[P, 4:5], in1=gsub[P, :], op0=ALU.mult, op1=ALU.mult),
                   [Bo, Bc], [Batt])

            ssb = A.alloc([128, 16]); rsb = A.alloc([128, 16])
            Bss = Buf("ssb")

            def combine_def(pb1, pb2, att_ap, col):
                op('dve', L('reciprocal', out=sm[:, 0:1], in_=ps[:, pb1, 128:129]), [PB[pb1]], [Bo])
                op('dve', L('reciprocal', out=sm[:, 1:2], in_=ps[:, pb2, 128:129]), [PB[pb2], Bo], [Bo])
                op('dve', L('tensor_tensor', out=sm[:, 1:2], in0=sm[:, 1:2], in1=lam_t[:, 1:2], op=ALU.mult), [Bo, Bc], [Bo])
                op('dve', L('tensor_scalar', out=o1n, in0=ps[:, pb1, 0:128], scalar1=sm[:, 0:1], scalar2=None, op0=ALU.mult), [PB[pb1], Bo], [Bo])
                op('dve', L('scalar_tensor_tensor', out=att_ap, in0=ps[:, pb2, 0:128], scalar=sm[:, 1:2], in1=o1n,
                            op0=ALU.mult, op1=ALU.add), [PB[pb2], Bo], [Batt])
                op('dve', L('scalar_tensor_tensor', out=junk, in0=att_ap, scalar=1.0, in1=att_ap, op0=ALU.mult, op1=ALU.mult,
                            accum_out=ssb[:, col:col + 1]), [Batt, Bss], [Bo, Bss])

            def finish_block():
                op('act', L('activation', out=rsb, in_=ssb, func=AF.Sqrt, bias=EPS, scale=1.0 / 128.0), [Bss], [Bss])
                op('dve', L('reciprocal', out=rsb, in_=rsb), [Bss], [Bss])
                for qq in range(4):
                    for hh in range(4):
                        a_ap = att[:, qq, hh * 128:(hh + 1) * 128]
                        op('dve', L('scalar_tensor_tensor', out=a_ap, in0=a_ap, scalar=rsb[:, qq * 4 + hh:qq * 4 + hh + 1], in1=gsub,
                                    op0=ALU.mult, op1=ALU.mult), [Bss, Bc, Batt], [Batt])

            if isP:
                PTs_ = [[A.alloc([128, 16, 512], BF16) for _ in range(2)] for _ in range(2)]
                BPTs_ = [[Buf("PT%d%d" % (a_, b_)) for b_ in range(2)] for a_ in range(2)]
                rot = [0]
                items = [(j, h) for j in range(4) for h in range(4)]

                def qk_exp(n):
                    j, h = items[n]
                    PT, BPT = PTs_[n % 2], BPTs_[n % 2]
                    nkt = 4 * (j + 1)
                    for p in range(2):
                        rows = slice(p * 64, (p + 1) * 64)
                        for kt in range(nkt):
                            off = max(0, kt - 4 * j) * 128
                            N = 512 - off
                            pb = rot[0] % 4
                            rot[0] += 1
                            mm(ps[:, pb, 0:N], kT[rows, h, kt * 128:(kt + 1) * 128], qT[rows, h, j * 512 + off:(j + 1) * 512], True, True,
                               [BkT, BqT], [PB[pb]])
                            op('act', L('activation', out=PT[p][:, kt, off:512], in_=ps[:, pb, 0:N], func=AF.Exp, scale=0.125), [PB[pb]], [BPT[p]])
                            if kt >= 4 * j:
                                op('pool', L('memset', PT[p][64:128, kt, off:off + 64], 0.0), (), [BPT[p]])

                def pv_combine(n):
                    j, h = items[n]
                    PT, BPT = PTs_[n % 2], BPTs_[n % 2]
                    for qq in range(4):
                        qt = 4 * j + qq
                        pbs = (4 + 2 * (qq % 2), 5 + 2 * (qq % 2))
                        for p in range(2):
                            for kt in range(qt + 1):
                                mm(ps[:, pbs[p], 0:130], PT[p][:, kt, qq * 128:(qq + 1) * 128], Vp[:, kt, h, :], kt == 0, kt == qt,
                                   [BPT[p], BVp], [PB[pbs[p]]])
                        combine_def(pbs[0], pbs[1], att[:, qq, h * 128:(h + 1) * 128], qq * 4 + h)

                qk_exp(0)
                for n in range(16):
                    j, h = items[n]
                    if n + 1 < 16:
                        qk_exp(n + 1)
                    if h == 0:
                        op('dve', L('memset', ssb, 0.0), [Bss], [Bss])
                    pv_combine(n)
                    if h == 3:
                        finish_block()
                        for qq in range(4):
                            qt = 4 * j + qq
                            pb = qq % 4
                            for hh in range(4):
                                tr(ps[:, pb, hh * 128:(hh + 1) * 128], att[:, qq, hh * 128:(hh + 1) * 128], 128, [Batt], [PB[pb]])
                            op('act', L('copy', out=mixT[:, 4:8, qt * 128:(qt + 1) * 128],
                                        in_=ps[:, pb, :].rearrange("p (h n) -> p h n", h=4)), [PB[pb]], [Bmix])
            else:
                PTs = [A.alloc([128, 16, 32], BF16) for _ in range(2)]
                PTn = A.alloc([32, 2, 32], BF16)
                BPT = [Buf("PTs0"), Buf("PTs1")]
                BPTn = Buf("PTn")
                kst = [A.alloc([128, 512]) for _ in range(4)]
                vst = [A.alloc([128, 512]) for _ in range(4)]
                Bkst = [Buf("kst%d" % i) for i in range(4)]
                Bvst = [Buf("vst%d" % i) for i in range(4)]
                for j in range(4):
                    for kt in range(16):
                        s = kt % 4
                        load(kst[s], ck[j * PAST + kt * 128:j * PAST + (kt + 1) * 128, :], Bkst[s])
                        load(vst[s], cv[j * PAST + kt * 128:j * PAST + (kt + 1) * 128, :], Bvst[s])
                        pb = kt % 2
                        for h in range(4):
                            tr(ps[:, pb, h * 128:(h + 1) * 128], kst[s][:, h * 128:(h + 1) * 128], 128, [Bkst[s]], [PB[pb]])
                        op('act', L('copy', out=kT[:, :, kt * 128:(kt + 1) * 128], in_=ps[:, pb, :].rearrange("p (h n) -> p h n", h=4)),
                           [PB[pb]], [BkT])
                        op('pool', L('tensor_copy', out=Vp[:, kt, :, 0:128], in_=vst[s].rearrange("p (h n) -> p h n", h=4)), [Bvst[s]], [BVp])
                    qc = slice(32 * j, 32 * j + 32)
                    for h in range(4):
                        for p in range(2):
                            rows = slice(p * 64, (p + 1) * 64)
                            pb = 2 + p
                            for kt in range(16):
                                mm(ps[:, pb, kt * 32:(kt + 1) * 32], kT[rows, h, kt * 128:(kt + 1) * 128], qT[rows, h, qc], True, True, [BkT, BqT], [PB[pb]])
                            op('act', L('activation', out=PTs[p], in_=ps[:, pb, :].rearrange("p (k q) -> p k q", k=16), func=AF.Exp, scale=0.125),
                               [PB[pb]], [BPT[p]])
                            mm(ps[0:32, 4, p * 32:(p + 1) * 32], kTn[rows, h, qc], qT[rows, h, qc], True, True, [BkTn, BqT], [PB[4]])
                        op('act', L('activation', out=PTn, in_=ps[0:32, 4, 0:64].rearrange("p (a q) -> p a q", a=2), func=AF.Exp, scale=0.125), [PB[4]], [BPTn])
                        for p in range(2):
                            pb = 5 + p
                            for kt in range(16):
                                mm(ps[0:32, pb, 0:130], PTs[p][:, kt, :], Vp[:, kt, h, :], kt == 0, False, [BPT[p], BVp], [PB[pb]])
                            mm(ps[0:32, pb, 0:130], PTn[:, p, :], Vn[:, j, h, :], False, True, [BPTn, BVn], [PB[pb]])
                        combine(32, 5, 6, att[0:32, 0, h * 128:(h + 1) * 128])
                    for h in range(4):
                        tr(ps[:, 7, h * 32:(h + 1) * 32], att[0:32, 0, h * 128:(h + 1) * 128], 32, [Batt], [PB[7]])
                    op('act', L('copy', out=mixT[:, 4:8, 32 * j:32 * j + 32], in_=ps[:, 7, 0:128].rearrange("p (h n) -> p h n", h=4)), [PB[7]], [Bmix])
            T.barrier()
            A.release()
            if STAGE < 4:
                A.release()
                continue

            A.mark()
            w_out_bf = A.alloc([128, 8, D], BF16)
            Bwo = Buf("w_out_bf")
            wst = [A.alloc([128, 8, 256]) for _ in range(2)]
            Bwst = [Buf("wst0"), Buf("wst1")]
            w_out_v = w_out.rearrange("(k p) n -> p k n", p=128)
            for cb in range(4):
                s = cb % 2
                load(wst[s], w_out_v[:, :, cb * 256:(cb + 1) * 256], Bwst[s])
                op('dve' if s == 0 else 'pool', L('tensor_copy', out=w_out_bf[:, :, cb * 256:(cb + 1) * 256], in_=wst[s]), [Bwst[s]], [Bwo])
            bout_bc = A.alloc([128, D]); l1g = A.alloc([128, D]); l1b = A.alloc([128, D]); g1bc = A.alloc([128, D])
            Bbc = Buf("bc1")
            load(bout_bc, b_out[0:1, :].partition_broadcast(128), Bbc)
            load(l1g, ln1_g[0:1, :].partition_broadcast(128), Bbc)
            load(l1b, ln1_b[0:1, :].partition_broadcast(128), Bbc)
            for (sq_, t0, n) in segs:
                npart = 128 if isP else 32
                p0 = 0 if isP else t0
                T.dma('sp', L('dma_start', out=g1bc[p0:p0 + npart, :],
                                                                               in_=modscr[sq_:sq_ + 1, 0:1024].partition_broadcast(npart)), [Bscr], [Bbc])
            xst = [A.alloc([128, D]) for _ in range(2)]
            Bxst = [Buf("xst0"), Buf("xst1")]
            tws = [A.alloc([128, D]) for _ in range(2)]; rws = [A.alloc([128, D]) for _ in range(2)]
            x1w = [A.alloc([128, D]) for _ in range(2)]
            sm1s = [A.alloc([128, 8]) for _ in range(2)]
            Bts, Brs, Bx1w, Bsms = [Buf("tw0"), Buf("tw1")], [Buf("rw0"), Buf("rw1")], [Buf("x1w0"), Buf("x1w1")], [Buf("sm10"), Buf("sm11")]
            tw, rw, sm1, Bt, Br, Bsm = tws[0], rws[0], sm1s[0], Bts[0], Brs[0], Bsms[0]

            def layer_norm_tail(rw_, Br_, s1col, out_ap, Bout, gbc, bbc, Bg):
                return ln_tail(sm1, Bsm, tw, Bt, rw_, Br_, out_ap, Bout, gbc, bbc, Bg)

            def _unused(rw_, Br_, s1col, out_ap, Bout, gbc, bbc, Bg):
                op('dve', L('memset', sm1[:, 1:2], 0.0), [Bsm], [Bsm])
                op('dve', L('scalar_tensor_tensor', out=tw, in0=rw_, scalar=1.0, in1=rw_, op0=ALU.mult, op1=ALU.mult, accum_out=sm1[:, 1:2]),
                   [Br_, Bsm], [Bt, Bsm])
                op('dve', L('tensor_scalar', out=sm1[:, 2:3], in0=sm1[:, s1col:s1col + 1], scalar1=-1.0 / D, scalar2=None, op0=ALU.mult), [Bsm], [Bsm])
                op('dve', L('tensor_tensor', out=sm1[:, 3:4], in0=sm1[:, 2:3], in1=sm1[:, 2:3], op=ALU.mult), [Bsm], [Bsm])
                op('dve', L('scalar_tensor_tensor', out=sm1[:, 4:5], in0=sm1[:, 1:2], scalar=1.0 / D, in1=sm1[:, 3:4], op0=ALU.mult, op1=ALU.subtract),
                   [Bsm], [Bsm])
                op('act', L('activation', out=sm1[:, 5:6], in_=sm1[:, 4:5], func=AF.Sqrt, bias=EPS, scale=1.0), [Bsm], [Bsm])
                op('dve', L('reciprocal', out=sm1[:, 6:7], in_=sm1[:, 5:6]), [Bsm], [Bsm])
                op('dve', L('tensor_scalar', out=tw, in0=rw_, scalar1=sm1[:, 2:3], scalar2=sm1[:, 6:7], op0=ALU.add, op1=ALU.mult), [Br_, Bsm, Bt], [Bt])
                op('pool', L('tensor_tensor', out=tw, in0=tw, in1=gbc, op=ALU.mult), [Bt, Bg], [Bt])
                op('pool', L('tensor_tensor', out=out_ap, in0=tw, in1=bbc, op=ALU.add), [Bt, Bg], [Bout])

            load(xst[0], x_src[0:128, :], Bxst[0])
            for t in range(ntile):
                s = t % 2
                tw, rw, sm1, Bt, Br, Bsm = tws[s], rws[s], sm1s[s], Bts[s], Brs[s], Bsms[s]
                if t + 1 < ntile:
                    load(xst[1 - s], x_src[(t + 1) * 128:(t + 2) * 128, :], Bxst[1 - s])
                for cb in range(2):
                    for k in range(8):
                        mm(ps[:, cb, :], mixT[:, k, t * 128:(t + 1) * 128], w_out_bf[:, k, cb * 512:(cb + 1) * 512], k == 0, k == 7, [Bmix, Bwo], [PB[cb]])
                op('dve', L('tensor_tensor', out=tw.rearrange("p (a b) -> p a b", a=2), in0=ps[:, 0:2, :],
                                                    in1=bout_bc.rearrange("p (a b) -> p a b", a=2), op=ALU.add), [PB[0], PB[1], Bbc], [Bt])
                op('pool', L('tensor_tensor', out=tw, in0=tw, in1=g1bc, op=ALU.mult), [Bt, Bbc], [Bt])
                op('dve', L('memset', sm1[:, 0:1], 0.0), [Bsm], [Bsm])
                op('dve', L('scalar_tensor_tensor', out=rw, in0=xst[s], scalar=float(ALPHA), in1=tw, op0=ALU.mult, op1=ALU.add,
                                                                accum_out=sm1[:, 0:1]), [Bxst[s], Bt, Bsm], [Br, Bsm])
                ln_tail(sm1, Bsm, tw, Bt, rw, Br, x1w[s], Bx1w[s], l1g, l1b, Bbc)
                store(x1scr[(tile0 + t) * 128:(tile0 + t + 1) * 128, :], x1w[s], Bx1w[s], [Bx1[tile0 + t]])
            T.barrier()
            A.release()
            A.release()

        if STAGE >= 5:
            T.barrier()
            A.top = top_consts
            A.mark()
            wq_bf = A.alloc([128, 8, 2048], BF16)
            Bwq = Buf("wq_bf")
            skT = A.alloc([128, 16, 128], BF16)
            BskT = Buf("skT")
            A.mark()
            wst = [A.alloc([128, 8, 256]) for _ in range(2)]
            Bwst = [Buf("wst0"), Buf("wst1")]
            w_q_v = w_query.rearrange("(k p) n -> p k n", p=128)
            for cb in range(8):
                s = cb % 2
                load(wst[s], w_q_v[:, :, cb * 256:(cb + 1) * 256], Bwst[s])
                op('dve' if s == 0 else 'pool', L('tensor_copy', out=wq_bf[:, :, cb * 256:(cb + 1) * 256], in_=wst[s]), [Bwst[s]], [Bwq])
            skst = [A.alloc([128, 128]) for _ in range(2)]
            Bskst = [Buf("sk0"), Buf("sk1")]
            for hp in range(16):
                s = hp % 2
                load(skst[s], sub_keys[hp * 128:(hp + 1) * 128, :], Bskst[s])
                tr(ps[:, s, 0:128], skst[s], 128, [Bskst[s]], [PB[s]])
                op('act', L('copy', out=skT[:, hp, :], in_=ps[:, s, 0:128]), [PB[s]], [BskT])
            T.barrier()
            A.release()
            bcs = [[A.alloc([128, D]) for _ in range(3)] for _ in range(2)]
            Bbcs = [Buf("bc2a"), Buf("bc2b")]
            l2g = A.alloc([128, D]); l2b = A.alloc([128, D])
            Bl2 = Buf("l2")
            load(l2g, ln2_g[0:1, :].partition_broadcast(128), Bl2)
            load(l2b, ln2_b[0:1, :].partition_broadcast(128), Bl2)
            x1t = [A.alloc([128, D]) for _ in range(2)]
            Bx1t = [Buf("x1t0"), Buf("x1t1")]
            h2s = [A.alloc([128, D]) for _ in range(2)]; Bh2s = [Buf("h2a"), Buf("h2b")]
            h2T = A.alloc([128, D], BF16); Bh2T = Buf("h2T")
            qpT = A.alloc([128, 16, 128], BF16); BqpT = Buf("qpT")
            s_sb = A.alloc([128, 2048]); Bs = Buf("s_sb")
            rep = A.alloc([128, 256]); Brep = Buf("rep")
            sv = A.alloc([128, 16, 16]); si = A.alloc([128, 16, 16], U32); sif = A.alloc([128, 16, 16])
            Bsv, Bsi = Buf("sv"), Buf("si")
            cand = A.alloc([128, 8, 16, 16]); Bcand = Buf("cand")
            fv = A.alloc([128, 8, 16]); fp_ = A.alloc([128, 8, 16], U32); Bfv = Buf("fv")
            fiu = A.alloc([128, 8, 16], U32); fju = A.alloc([128, 8, 16], U32); fif = A.alloc([128, 8, 16]); fjf = A.alloc([128, 8, 16])
            Bfi = Buf("fi")
            oh = A.alloc([128, 8, 16, 16]); Boh = Buf("oh")
            e0 = A.alloc([128, 8, 16]); e1 = A.alloc([128, 8, 16]); ef = A.alloc([128, 128])
            eidxs = [A.alloc([128, 128], I32) for _ in range(2)]
            Be, Beidxs = Buf("e"), [Buf("eidxa"), Buf("eidxb")]
            gws = [A.alloc([128, 8, 16]) for _ in range(2)]; gs = A.alloc([128, 8]); Bgs = [Buf("gwa"), Buf("gwb")]
            av = A.alloc([128, 128]); wgt = A.alloc([128, 128]); Ba, Bwg = Buf("a"), Buf("wgt")
            junk2 = A.alloc([128, D]); Bj2 = Buf("junk2")
            acc = A.alloc([128, D]); Bacc = Buf("acc")
            tw = A.alloc([128, D]); rw = A.alloc([128, D]); yw = [A.alloc([128, D]) for _ in range(2)]
            sm1 = A.alloc([128, 8])
            Bt, Br, Byw, Bsm = Buf("tw"), Buf("rw"), [Buf("yw0"), Buf("yw1")], Buf("sm1")
            NG = min(NRING, (A.n - A.top) // 1024)
            print("PEER gather ring buffers:", NG, flush=True)
            ring = [A.alloc([128, D]) for _ in range(NG)]
            Bring = [Buf("ring%d" % i) for i in range(NG)]
            ring_state = [0]

            def bc_iota():
                return iota16.unsqueeze(1).unsqueeze(1).broadcast_to([128, 8, 16, 16])

            tiles = [(t, 'p', t // 16) for t in range(32)] + [(32, 's', 0)]
            if STAGE < 9:
                tiles = [(t, 'p', 0) for t in range(16)] + [(32, 's', 0)]
            seq_set = {}
            for (tg, kind, sidx) in tiles:
                if (kind, sidx) not in seq_set:
                    seq_set[(kind, sidx)] = len(seq_set) % 2
            loaded = set()

            def front(ti):
                tg, kind, sidx = tiles[ti]
                s = ti % 2
                bset = seq_set[(kind, sidx)]
                sc2bc, sh2bc, g2bc = bcs[bset]
                Bbc2 = Bbcs[bset]
                h2, Bh2 = h2s[s], Bh2s[s]
                eidx, Beidx = eidxs[s], Beidxs[s]
                gw, Bg = gws[s], Bgs[s]
                if (kind, sidx) not in loaded:
                    loaded.add((kind, sidx))
                    if kind == 'p':
                        rows = [(sidx, 0, 128)]
                    else:
                        rows = [(2 + j, 32 * j, 32) for j in range(4)]
                    for (sq_, p0, npart) in rows:
                        T.dma('sp', L('dma_start', out=sh2bc[p0:p0 + npart, :], in_=modscr[sq_:sq_ + 1, 1024:2048].partition_broadcast(npart)), [Bscr], [Bbc2])
                        T.dma('sp', L('dma_start', out=sc2bc[p0:p0 + npart, :], in_=modscr[sq_:sq_ + 1, 2048:3072].partition_broadcast(npart)), [Bscr], [Bbc2])
                        T.dma('sp', L('dma_start', out=g2bc[p0:p0 + npart, :], in_=modscr[sq_:sq_ + 1, 3072:4096].partition_broadcast(npart)), [Bscr], [Bbc2])
                T.dma('sp', L('dma_start', out=x1t[s], in_=x1scr[tg * 128:(tg + 1) * 128, :]), [Bx1[tg]], [Bx1t[s]])
                op('dve', L('tensor_tensor', out=h2, in0=x1t[s], in1=sc2bc, op=ALU.mult), [Bx1t[s], Bbc2], [Bh2])
                op('dve', L('tensor_tensor', out=h2, in0=h2, in1=sh2bc, op=ALU.add), [Bh2, Bbc2], [Bh2])
                yield
                for c in range(8):
                    tr(ps[:, c // 4, (c % 4) * 128:(c % 4 + 1) * 128], h2[:, c * 128:(c + 1) * 128], 128, [Bh2], [PB[c // 4]])
                op('act', L('copy', out=h2T.rearrange("p (a b) -> p a b", a=2), in_=ps[:, 0:2, :]), [PB[0], PB[1]], [Bh2T])
                for g4 in range(4):
                    bank = 2 + g4 % 2
                    for i in range(4):
                        hp = g4 * 4 + i
                        for k in range(8):
                            mm(ps[:, bank, i * 128:(i + 1) * 128], wq_bf[:, k, hp * 128:(hp + 1) * 128], h2T[:, k * 128:(k + 1) * 128], k == 0, k == 7,
                               [Bwq, Bh2T], [PB[bank]])
                    op('act', L('copy', out=qpT[:, g4 * 4:(g4 + 1) * 4, :], in_=ps[:, bank, :].rearrange("p (a b) -> p a b", a=4)), [PB[bank]], [BqpT])
                yield
                for g4 in range(4):
                    bank = 4 + g4 % 2
                    for i in range(4):
                        hp = g4 * 4 + i
                        mm(ps[:, bank, i * 128:(i + 1) * 128], qpT[:, hp, :], skT[:, hp, :], True, True, [BqpT, BskT], [PB[bank]])
                    op('act', L('copy', out=s_sb[:, g4 * 512:(g4 + 1) * 512], in_=ps[:, bank, :]), [PB[bank]], [Bs])
                yield
                for hp in range(16):
                    sl = s_sb[:, hp * 128:(hp + 1) * 128]
                    op('dve', L('max', out=sv[:, hp, 0:8], in_=sl), [Bs], [Bsv])
                    op('dve', L('match_replace', out=rep[:, 0:128], in_to_replace=sv[:, hp, 0:8], in_values=sl, imm_value=-1e30), [Bs, Bsv], [Brep])
                    op('dve', L('max', out=sv[:, hp, 8:16], in_=rep[:, 0:128]), [Brep], [Bsv])
                    op('dve', L('max_index', out=si[:, hp, 0:8], in_max=sv[:, hp, 0:8], in_values=sl), [Bs, Bsv], [Bsi])
                    op('dve', L('max_index', out=si[:, hp, 8:16], in_max=sv[:, hp, 8:16], in_values=sl), [Bs, Bsv], [Bsi])
                    yield
                op('dve', L('tensor_copy', out=sif, in_=si), [Bsi], [Bsi])
                svv = sv.rearrange("p (h two) i -> p h two i", two=2)
                sfv = sif.rearrange("p (h two) i -> p h two i", two=2)
                op('pool', L('tensor_tensor', out=cand, in0=svv[:, :, 0, :].unsqueeze(3).broadcast_to([128, 8, 16, 16]),
                             in1=svv[:, :, 1, :].unsqueeze(2).broadcast_to([128, 8, 16, 16]), op=ALU.add), [Bsv], [Bcand])
                yield
                for h in range(8):
                    ch_ = cand[:, h, :, :].rearrange("p a b -> p (a b)")
                    op('dve', L('max', out=fv[:, h, 0:8], in_=ch_), [Bcand], [Bfv])
                    op('dve', L('match_replace', out=rep, in_to_replace=fv[:, h, 0:8], in_values=ch_, imm_value=-1e30), [Bcand, Bfv], [Brep])
                    op('dve', L('max', out=fv[:, h, 8:16], in_=rep), [Brep], [Bfv])
                    op('dve', L('max_index', out=fp_[:, h, 0:8], in_max=fv[:, h, 0:8], in_values=ch_), [Bcand, Bfv], [Bfv])
                    op('dve', L('max_index', out=fp_[:, h, 8:16], in_max=fv[:, h, 8:16], in_values=ch_), [Bcand, Bfv], [Bfv])
                    yield
                op('dve', L('tensor_single_scalar', out=fiu, in_=fp_, scalar=4, op=ALU.logical_shift_right), [Bfv], [Bfi])
                op('dve', L('tensor_single_scalar', out=fju, in_=fp_, scalar=15, op=ALU.bitwise_and), [Bfv], [Bfi])
                op('dve', L('tensor_copy', out=fif, in_=fiu), [Bfi], [Bfi])
                op('dve', L('tensor_copy', out=fjf, in_=fju), [Bfi], [Bfi])
                yield
                for (ff_, col, eo) in ((fif, 0, e0), (fjf, 1, e1)):
                    op('dve', L('tensor_tensor', out=oh, in0=bc_iota(), in1=ff_.unsqueeze(3).broadcast_to([128, 8, 16, 16]), op=ALU.is_equal), [Bfi, Bc], [Boh])
                    op('dve', L('tensor_tensor', out=oh, in0=oh, in1=sfv[:, :, col, :].unsqueeze(2).broadcast_to([128, 8, 16, 16]), op=ALU.mult), [Boh, Bsi], [Boh])
                    op('dve', L('tensor_reduce', out=eo, in_=oh, axis=AX.X, op=ALU.add), [Boh], [Be])
                    yield
                op('dve', L('scalar_tensor_tensor', out=ef.rearrange("p (h k) -> p h k", h=8), in0=e0, scalar=128.0, in1=e1, op0=ALU.mult, op1=ALU.add), [Be], [Be])
                op('dve', L('tensor_copy', out=eidx, in_=ef), [Be], [Beidx])
                op('dve', L('tensor_tensor', out=gw, in0=fv, in1=fv[:, :, 0:1].broadcast_to([128, 8, 16]), op=ALU.subtract), [Bfv], [Bg])
                op('act', L('activation', out=gw, in_=gw, func=AF.Exp), [Bg], [Bg])
                op('dve', L('tensor_reduce', out=gs, in_=gw, axis=AX.X, op=ALU.add), [Bg], [Bg])
                op('dve', L('reciprocal', out=gs, in_=gs), [Bg], [Bg])
                op('dve', L('tensor_tensor', out=gw, in0=gw, in1=gs.unsqueeze(2).broadcast_to([128, 8, 16]), op=ALU.mult), [Bg], [Bg])
                yield

            def gather(tab, eidx, Beidx, slot):
                ri = ring_state[0] % NG
                ring_state[0] += 1
                T.dma('pool', L('indirect_dma_start', out=ring[ri], out_offset=None, in_=tab,
                                in_offset=bass.IndirectOffsetOnAxis(ap=eidx[:, slot:slot + 1], axis=0)), [Beidx], [Bring[ri]])
                return ri

            g0 = front(0)
            for _ in g0:
                pass
            for ti in range(len(tiles)):
                tg, kind, sidx = tiles[ti]
                s = ti % 2
                bset = seq_set[(kind, sidx)]
                g2bc, Bbc2 = bcs[bset][2], Bbcs[bset]
                h2, Bh2 = h2s[s], Bh2s[s]
                eidx, Beidx = eidxs[s], Beidxs[s]
                gw, Bg = gws[s], Bgs[s]
                nxt = front(ti + 1) if ti + 1 < len(tiles) else None
                if PEER_ON:
                    op('dve', L('memset', av, 0.0), [Ba], [Ba])
                    for slot in range(128):
                        ri = gather(u_tab, eidx, Beidx, slot)
                        op('dve', L('scalar_tensor_tensor', out=junk2, in0=ring[ri], scalar=1.0, in1=h2, op0=ALU.mult, op1=ALU.mult,
                                    accum_out=av[:, slot:slot + 1]), [Bring[ri], Bh2, Ba], [Bj2, Ba])
                    op('act', L('activation', out=wgt, in_=av, func=AF.Gelu), [Ba], [Bwg])
                    op('dve', L('tensor_tensor', out=wgt, in0=wgt, in1=gw.rearrange("p h k -> p (h k)"), op=ALU.mult), [Bwg, Bg], [Bwg])
                    for slot in range(128):
                        ri = gather(v_tab, eidx, Beidx, slot)
                        if slot == 0:
                            op('dve', L('tensor_scalar', out=acc, in0=ring[ri], scalar1=wgt[:, 0:1], scalar2=None, op0=ALU.mult), [Bring[ri], Bwg], [Bacc])
                        else:
                            op('dve', L('scalar_tensor_tensor', out=acc, in0=ring[ri], scalar=wgt[:, slot:slot + 1], in1=acc, op0=ALU.mult, op1=ALU.add),
                               [Bring[ri], Bwg, Bacc], [Bacc])
                        if nxt is not None and slot % 3 == 2:
                            next(nxt, None)
                    op('pool', L('tensor_tensor', out=tw, in0=acc, in1=g2bc, op=ALU.mult), [Bacc, Bbc2], [Bt])
                    if DEBUG and tg == 0:
                        store(dbg_h2, h2, Bh2); store(dbg_ff, acc, Bacc); store(dbg_e, eidx, Beidx); store(dbg_g, gw.rearrange("p h k -> p (h k)"), Bg)
                else:
                    op('pool', L('memset', tw, 0.0), [Bt], [Bt])
                if nxt is not None:
                    for _ in nxt:
                        pass
                op('dve', L('memset', sm1[:, 0:1], 0.0), [Bsm], [Bsm])
                op('dve', L('scalar_tensor_tensor', out=rw, in0=x1t[s], scalar=float(ALPHA), in1=tw, op0=ALU.mult, op1=ALU.add,
                            accum_out=sm1[:, 0:1]), [Bx1t[s], Bt, Bsm], [Br, Bsm])
                ln_tail(sm1, Bsm, tw, Bt, rw, Br, yw[s], Byw[s], l2g, l2b, Bl2)
                ydst = y_p[tg * 128:(tg + 1) * 128, :] if kind == 'p' else y_s
                store(ydst, yw[s], Byw[s])
            A.release()

        T.finish()
        with nc.Block() as block:
            T.emit(block)
    print("ops:", T.nops, flush=True)
    return nc


def _rope_tables():
    half = 32
    inv = (1.0 / (np.float32(10000.0) ** (np.arange(half, dtype=np.float32) / np.float32(half)))).astype(np.float32)
    posp = np.arange(SEQ, dtype=np.float32)
    angp = (posp[:, None] * inv[None, :]).astype(np.float32)
    poss = (PAST + (np.arange(128) % DSEQ)).astype(np.float32)
    angs = (poss[:, None] * inv[None, :]).astype(np.float32)
    return (np.cos(angp).astype(np.float32), np.sin(angp).astype(np.float32),
            np.cos(angs).astype(np.float32), np.sin(angs).astype(np.float32))


_NC_CACHE = {}


def kernel(x_prompt, x_sample, cache_k, cache_v, cache_conv, c_prompt, c_sample,
           w_ada, b_ada, w_in, b_in, w_dw, b_dw, conv_ln_g, conv_ln_b,
           lam_q1, lam_k1, lam_q2, lam_k2, subln_g, w_out, b_out, ln1_g, ln1_b,
           w_query, sub_keys, u_tab, v_tab, ln2_g, ln2_b):
    f = lambda a: np.ascontiguousarray(np.asarray(a, dtype=np.float32))
    x_prompt, x_sample, cache_k, cache_v, cache_conv = f(x_prompt), f(x_sample), f(cache_k), f(cache_v), f(cache_conv)
    c_prompt, c_sample = f(c_prompt), f(c_sample)
    cosp, sinp, coss, sins = _rope_tables()
    shared = {
        "w_ada": f(w_ada)[0], "b_ada": f(b_ada), "w_in": f(w_in)[0], "b_in": f(b_in), "w_dw": f(w_dw)[0], "b_dw": f(b_dw),
        "cln_g": f(conv_ln_g), "cln_b": f(conv_ln_b),
        "lamv": np.ascontiguousarray(np.concatenate([f(lam_q1), f(lam_k1), f(lam_q2), f(lam_k2)], axis=0)),
        "subln_g": f(subln_g), "w_out": f(w_out)[0], "b_out": f(b_out), "ln1_g": f(ln1_g), "ln1_b": f(ln1_b),
        "w_query": f(w_query)[0], "sub_keys": f(sub_keys).reshape(16 * 128, 128), "u_tab": f(u_tab)[0], "v_tab": f(v_tab)[0],
        "ln2_g": f(ln2_g), "ln2_b": f(ln2_b),
        "ident": np.eye(128, dtype=np.float32), "iota16": np.tile(np.arange(16, dtype=np.float32)[None, :], (128, 1)),
        "cosp": cosp, "sinp": sinp, "coss": coss, "sins": sins,
    }
    in_maps = []
    for i in range(NCORES):
        m = dict(shared)
        m["xp"] = x_prompt[NPS * i:NPS * (i + 1)].reshape(NPS * SEQ, D)
        m["xs"] = x_sample[NSS * i:NSS * (i + 1)].reshape(128, D)
        m["ck"] = cache_k[0, NSS * i:NSS * (i + 1)].reshape(NSS * PAST, 512)
        m["cv"] = cache_v[0, NSS * i:NSS * (i + 1)].reshape(NSS * PAST, 512)
        m["cc"] = cache_conv[0, NSS * i:NSS * (i + 1)].reshape(NSS * HIST, 512)
        m["c6"] = np.ascontiguousarray(np.concatenate([c_prompt[NPS * i:NPS * (i + 1)], c_sample[NSS * i:NSS * (i + 1)]], axis=0))
        in_maps.append(m)
    if "nc" not in _NC_CACHE:
        _NC_CACHE["nc"] = build_program()
    nc = _NC_CACHE["nc"]
    res = run_bass_kernel_spmd(nc, in_maps, core_ids=list(range(NCORES)))
    R = res.results
    if DEBUG:
        _NC_CACHE["dbg"] = {k: np.asarray(R[0][k]) for k in ("dbg_h2", "dbg_ff", "dbg_e", "dbg_g")}
    cat = lambda name: np.concatenate([np.asarray(R[i][name]) for i in range(NCORES)], axis=0)
    y_prompt = cat("y_p").reshape(16, SEQ, D)
    y_sample = cat("y_s").reshape(32, DSEQ, D)
    k_prompt = cat("k_p").reshape(1, 16, SEQ, 4, 2, 64)
    v_prompt = cat("v_p").reshape(1, 16, SEQ, 4, 128)
    conv_prompt = cat("conv_p").reshape(1, 16, HIST, 512)
    k_sample = cat("k_s").reshape(1, 32, DSEQ, 4, 2, 64)
    v_sample = cat("v_s").reshape(1, 32, DSEQ, 4, 128)
    conv_sample = cat("conv_s").reshape(1, 32, HIST, 512)
    return (y_prompt, y_sample, k_prompt, v_prompt, conv_prompt, k_sample, v_sample, conv_sample)
```
